# Optimizing a Trainium2 kernel written in Bass

```python
import jax, jax.numpy as jnp
from jax import lax
import numpy as np

D_MODEL = 1024
BATCH = 32
SEQ = 2048
DEPTH = 1

CHUNK = 64
Q_BLOCK = 128
N_HEADS = 8
QK_NOPE_DIM = 64
QK_ROPE_DIM = 32
QK_HEAD_DIM = QK_NOPE_DIM + QK_ROPE_DIM
V_HEAD_DIM = 64
Q_LORA_RANK = 256
KV_LORA_RANK = 128
MLA_WIDTH = N_HEADS * V_HEAD_DIM
CONV_CH = 512
CONV_WIDTH = 31
D_FF = 4 * D_MODEL
N_BRANCH = 2
ADA_CHUNKS = 6
ROPE_THETA = 10000.0
EPS = 1e-6

OFF_Q = Q_LORA_RANK
OFF_KV = OFF_Q + KV_LORA_RANK
OFF_KR = OFF_KV + QK_ROPE_DIM
OFF_GLU = OFF_KR + 2 * CONV_CH
D_IN = OFF_GLU + N_BRANCH * D_MODEL

kernel_name = "hybrid_mla_conformer_conv_adaln_block"


def rms_norm(x, g):
    xf = x.astype(jnp.float32)
    y = xf * lax.rsqrt(jnp.mean(jnp.square(xf), axis=-1, keepdims=True) + EPS)
    return (y * g.astype(jnp.float32)).astype(x.dtype)


def layer_norm(x, g, b):
    xf = x.astype(jnp.float32)
    mu = jnp.mean(xf, axis=-1, keepdims=True)
    var = jnp.mean(jnp.square(xf - mu), axis=-1, keepdims=True)
    y = (xf - mu) * lax.rsqrt(var + EPS)
    return (y * g.astype(jnp.float32) + b.astype(jnp.float32)).astype(x.dtype)


def rope_tables(seq, dtype):
    inv_freq = ROPE_THETA ** (-jnp.arange(0, QK_ROPE_DIM, 2, dtype=jnp.float32) / QK_ROPE_DIM)
    ang = jnp.arange(seq, dtype=jnp.float32)[:, None] * inv_freq[None, :]
    return jnp.cos(ang)[:, None, :].astype(dtype), jnp.sin(ang)[:, None, :].astype(dtype)


def apply_rope(x, cos, sin):
    half = x.shape[-1] // 2
    x1, x2 = x[..., :half], x[..., half:]
    return jnp.concatenate([x1 * cos - x2 * sin, x2 * cos + x1 * sin], axis=-1)


def chunk_causal_attention(q, k, v):
    seq = q.shape[1]
    scale = QK_HEAD_DIM ** -0.5
    chunk_id = jnp.arange(seq) // CHUNK
    outs = []
    for q0 in range(0, seq, Q_BLOCK):
        kv_end = q0 + Q_BLOCK
        qb = q[:, q0:kv_end]
        kb = k[:, :kv_end]
        vb = v[:, :kv_end]
        s = jnp.einsum('bqhd,bkhd->bhqk', qb, kb).astype(jnp.float32) * scale
        mask = chunk_id[q0:kv_end][:, None] >= chunk_id[:kv_end][None, :]
        s = jnp.where(mask[None, None], s, jnp.finfo(jnp.float32).min)
        p = jax.nn.softmax(s, axis=-1).astype(v.dtype)
        outs.append(jnp.einsum('bhqk,bkhd->bqhd', p, vb))
    return jnp.concatenate(outs, axis=1)


def causal_depthwise_conv(u, w, b):
    out = lax.conv_general_dilated(
        u, w[:, None, :].astype(u.dtype), window_strides=(1,),
        padding=[(CONV_WIDTH - 1, 0)],
        dimension_numbers=('NWC', 'WIO', 'NWC'),
        feature_group_count=u.shape[-1])
    return out + b


def setup_inputs(seed: int = 0) -> dict:
    key = jax.random.key(seed)
    ks = jax.random.split(key, 24)
    f32 = jnp.float32
    L = DEPTH

    def nrm(k, shape, fan_in):
        return jax.random.normal(k, shape, f32) * (fan_in ** -0.5)

    def gain(k, shape):
        return 1.0 + 0.02 * jax.random.normal(k, shape, f32)

    return {
        "x": jax.random.normal(ks[0], (BATCH, SEQ, D_MODEL), f32),
        "c": jax.random.normal(ks[1], (BATCH, D_MODEL), f32),
        "w_ada": nrm(ks[2], (L, D_MODEL, ADA_CHUNKS * D_MODEL), D_MODEL),
        "b_ada": 0.02 * jax.random.normal(ks[3], (L, ADA_CHUNKS * D_MODEL), f32),
        "norm1_g": gain(ks[4], (L, D_MODEL)),
        "w_in": nrm(ks[5], (L, D_MODEL, D_IN), D_MODEL),
        "q_latent_g": gain(ks[6], (L, Q_LORA_RANK)),
        "w_uq": nrm(ks[7], (L, Q_LORA_RANK, N_HEADS * QK_HEAD_DIM), Q_LORA_RANK),
        "kv_latent_g": gain(ks[8], (L, KV_LORA_RANK)),
        "w_ukv": nrm(ks[9], (L, KV_LORA_RANK, N_HEADS * (QK_NOPE_DIM + V_HEAD_DIM)), KV_LORA_RANK),
        "qk_norm_q_g": gain(ks[10], (L, QK_HEAD_DIM)),
        "qk_norm_k_g": gain(ks[11], (L, QK_HEAD_DIM)),
        "w_o_mla": nrm(ks[12], (L, MLA_WIDTH, D_MODEL), MLA_WIDTH),
        "conv_w": nrm(ks[13], (L, CONV_WIDTH, CONV_CH), CONV_WIDTH),
        "conv_b": 0.02 * jax.random.normal(ks[14], (L, CONV_CH), f32),
        "conv_ln_g": gain(ks[15], (L, CONV_CH)),
        "conv_ln_b": 0.02 * jax.random.normal(ks[16], (L, CONV_CH), f32),
        "w_pw_out": nrm(ks[17], (L, CONV_CH, D_MODEL), CONV_CH),
        "w_out": nrm(ks[18], (L, D_MODEL, D_MODEL), D_MODEL),
        "norm2_g": gain(ks[19], (L, D_MODEL)),
        "w_ff1": nrm(ks[20], (L, D_MODEL, D_FF), D_MODEL),
        "w_ff2": nrm(ks[21], (L, D_FF, D_MODEL), D_FF),
    }


def reference(x, c, w_ada, b_ada, norm1_g, w_in, q_latent_g, w_uq, kv_latent_g, w_ukv,
              qk_norm_q_g, qk_norm_k_g, w_o_mla, conv_w, conv_b, conv_ln_g, conv_ln_b,
              w_pw_out, w_out, norm2_g, w_ff1, w_ff2):
    B, S, D = x.shape
    cos, sin = rope_tables(S, x.dtype)
    c_act = jax.nn.silu(c)
    for l in range(DEPTH):
        mod = c_act @ w_ada[l] + b_ada[l]
        shift1, scale1, gate1, shift2, scale2, gate2 = jnp.split(mod[:, None, :], ADA_CHUNKS, axis=-1)

        h = rms_norm(x, norm1_g[l]) * (1.0 + scale1) + shift1
        z = h @ w_in[l]
        z_q = z[..., :OFF_Q]
        z_kv = z[..., OFF_Q:OFF_KV]
        z_kr = z[..., OFF_KV:OFF_KR]
        z_glu = z[..., OFF_KR:OFF_GLU]
        z_gate = z[..., OFF_GLU:]

        q = (rms_norm(z_q, q_latent_g[l]) @ w_uq[l]).reshape(B, S, N_HEADS, QK_HEAD_DIM)
        kv = (rms_norm(z_kv, kv_latent_g[l]) @ w_ukv[l]).reshape(B, S, N_HEADS, QK_NOPE_DIM + V_HEAD_DIM)
        k_nope, v = kv[..., :QK_NOPE_DIM], kv[..., QK_NOPE_DIM:]
        k_rope = jnp.broadcast_to(z_kr[:, :, None, :], (B, S, N_HEADS, QK_ROPE_DIM))
        k = jnp.concatenate([k_nope, k_rope], axis=-1)
        q = rms_norm(q, qk_norm_q_g[l])
        k = rms_norm(k, qk_norm_k_g[l])
        q = jnp.concatenate([q[..., :QK_NOPE_DIM], apply_rope(q[..., QK_NOPE_DIM:], cos, sin)], axis=-1)
        k = jnp.concatenate([k[..., :QK_NOPE_DIM], apply_rope(k[..., QK_NOPE_DIM:], cos, sin)], axis=-1)
        attn = chunk_causal_attention(q, k, v).reshape(B, S, MLA_WIDTH)
        y_a = attn @ w_o_mla[l]

        glu_a, glu_b = jnp.split(z_glu, 2, axis=-1)
        u = glu_a * jax.nn.sigmoid(glu_b)
        u = causal_depthwise_conv(u, conv_w[l], conv_b[l])
        u = jax.nn.silu(layer_norm(u, conv_ln_g[l], conv_ln_b[l]))
        y_b = u @ w_pw_out[l]

        g_a, g_b = jnp.split(jax.nn.sigmoid(z_gate), N_BRANCH, axis=-1)
        mixed = (g_a * y_a + g_b * y_b) @ w_out[l]
        x = x + gate1 * mixed

        h2 = rms_norm(x, norm2_g[l]) * (1.0 + scale2) + shift2
        f = jnp.square(jax.nn.relu(h2 @ w_ff1[l])) @ w_ff2[l]
        x = x + gate2 * f
    return x
```

```python
import contextlib
import numpy as np
import concourse.bass as bass
import concourse.mybir as mybir
from concourse.bass_utils import run_bass_kernel_spmd

F32 = mybir.dt.float32
BF16 = mybir.dt.bfloat16
AF = mybir.ActivationFunctionType
ALU = mybir.AluOpType
AX = mybir.AxisListType

COMPUTE = ("pe", "act", "dve", "pool")
QUEUES = ("pe", "act", "dve", "pool", "sp")

NCORES = 8
D = 1024
S = 2048
BPC = 4
T = 512
NT = S // T
NH = 8
EPS = 1e-6
PANEL = 4096
RING = 3


class Prog:
    def __init__(self, nc, epoch_len=2000):
        self.nc = nc
        self.ops = []
        self.per_q = {q: [] for q in QUEUES}
        self.last_w = {}
        self.readers = {}
        self.stream_last = {}
        self.stream_cnt = {}
        self.epoch_len = epoch_len

    def _deps(self, reads, writes, me):
        deps = set()
        for r in reads:
            w = self.last_w.get(r)
            if w is not None:
                deps.add(w)
        for w_ in writes:
            w = self.last_w.get(w_)
            if w is not None:
                deps.add(w)
            for rd in self.readers.get(w_, ()):
                deps.add(rd)
        deps.discard(me)
        return deps

    def _commit(self, reads, writes, me):
        mo = self.ops[me]
        for r in reads:
            lst = self.readers.setdefault(r, [])
            if mo["kind"] == "c":
                lst[:] = [x for x in lst if not (self.ops[x]["kind"] == "c" and self.ops[x]["q"] == mo["q"])]
            lst.append(me)
        for w_ in writes:
            self.last_w[w_] = me
            self.readers[w_] = []

    def op(self, q, fn, reads=(), writes=()):
        idx = len(self.ops)
        deps = self._deps(reads, writes, idx)
        if q == "pe":
            deps = {d for d in deps if not (self.ops[d]["kind"] == "c" and self.ops[d]["q"] == "pe")}
        self.ops.append(dict(kind="c", q=q, fn=fn, deps=deps, sig=False))
        self._commit(reads, writes, idx)
        return idx

    def dma(self, q, stream, fn, reads=(), writes=()):
        idx = len(self.ops)
        deps = self._deps(reads, writes, idx)
        prev = self.stream_last.get(stream)
        if prev is not None:
            deps.add(prev)
        n = self.stream_cnt.get(stream, 0) + 1
        self.stream_cnt[stream] = n
        self.stream_last[stream] = idx
        self.ops.append(dict(kind="d", q=q, fn=fn, deps=deps, stream=stream, val=16 * n, sig=True))
        self._commit(reads, writes, idx)
        return idx

    def emit(self, final_wait_ops=()):
        nc = self.nc
        ops = self.ops
        for o in ops:
            for d in o["deps"]:
                if ops[d]["kind"] == "c":
                    ops[d]["sig"] = True
        signum = {}
        cnt = {q: 0 for q in COMPUTE}
        for i, o in enumerate(ops):
            if o["kind"] == "c" and o["sig"]:
                cnt[o["q"]] += 1
                signum[i] = cnt[o["q"]]
        stack = contextlib.ExitStack()
        EL = self.epoch_len
        sems = {}
        for q in COMPUTE:
            ne = cnt[q] // EL + 1
            sems[q] = [stack.enter_context(nc.semaphore(f"s_{q}_{e}")) for e in range(ne)]
        ssem = {s: stack.enter_context(nc.semaphore(f"d_{s}")) for s in self.stream_cnt}
        clock = {q: {} for q in QUEUES}
        snap = {}
        plans = {q: [] for q in QUEUES}
        for i, o in enumerate(ops):
            q = o["q"]
            ck = clock[q]
            wm = {}
            for d in sorted(o["deps"]):
                od = ops[d]
                if od["kind"] == "c":
                    key, v = od["q"], signum[d]
                else:
                    key, v = "S:" + od["stream"], od["val"]
                if ck.get(key, 0) >= v:
                    continue
                wm[key] = max(wm.get(key, 0), v)
                for k2, v2 in snap[d].items():
                    if ck.get(k2, 0) < v2:
                        ck[k2] = v2
                if ck.get(key, 0) < v:
                    ck[key] = v
            s = dict(ck)
            if o["kind"] == "c":
                if o["sig"]:
                    s[q] = max(s.get(q, 0), signum[i])
            else:
                s["S:" + o["stream"]] = o["val"]
            snap[i] = s
            plans[q].append((i, wm))
        self.stats = dict(cnt=cnt, nops=len(ops), nsem=sum(len(v) for v in sems.values()) + len(ssem))

        def run(q, eng):
            for i, wm in plans[q]:
                o = ops[i]
                for k, v in wm.items():
                    if k.startswith("S:"):
                        eng.wait_ge(ssem[k[2:]], v)
                    else:
                        e = (v - 1) // EL
                        eng.wait_ge(sems[k][e], v - e * EL)
                ins = o["fn"](eng)
                if o["kind"] == "d":
                    ins.then_inc(ssem[o["stream"]], 16)
                elif o["sig"]:
                    e = (signum[i] - 1) // EL
                    ins.then_inc(sems[q][e], 1)
            if q == "sp":
                for d in final_wait_ops:
                    od = ops[d]
                    eng.wait_ge(ssem[od["stream"]], od["val"])

        with stack:
            with nc.Block() as block:
                @block.tensor
                def _(e):
                    run("pe", e)

                @block.scalar
                def _(e):
                    run("act", e)

                @block.vector
                def _(e):
                    run("dve", e)

                @block.gpsimd
                def _(e):
                    run("pool", e)

                @block.sync
                def _(e):
                    run("sp", e)


def panel_list():
    pl = []
    pl.append(("lat", "cols", ("w_in", 8, 0, 416)))
    pl.append(("uqkv", "uqkv", None))
    pl.append(("gluA", "cols", ("w_in", 8, 416, 512)))
    pl.append(("gluB", "cols", ("w_in", 8, 928, 512)))
    for g in range(4):
        pl.append((f"gate{g}", "cols", ("w_in", 8, 1440 + 512 * g, 512)))
    for cc in range(4):
        pl.append((f"cdiag{cc}", "cdiag", cc))
    pl.append(("pw", "cols", ("w_pw_out", 4, 0, 1024)))
    pl.append(("wo0", "wo", 0))
    pl.append(("wo1", "wo", 1))
    pl.append(("wout0", "cols", ("w_out", 8, 0, 512)))
    pl.append(("wout1", "cols", ("w_out", 8, 512, 512)))
    for dh in range(2):
        for p in range(4):
            pl.append((f"ff1_{dh}_{p}", "cols", ("w_ff1", 8, 2048 * dh + 512 * p, 512)))
        for p in range(4):
            pl.append((f"ff2_{dh}_{p}", "rows", ("w_ff2", 16 * dh + 4 * p, 4, 1024)))
    return pl


def build_nc(BPC=BPC, NT=NT):
    nc = bass.Bass("TRN2", target_bir_lowering=False)
    P = Prog(nc)

    def din(name, shape, dt=F32):
        return nc.dram_tensor(name, list(shape), dt, kind="ExternalInput").ap()

    x_d = din("x", [BPC, S, D])
    c_d = din("c", [BPC, D])
    w_ada = din("w_ada", [D, 6 * D])
    b_ada = din("b_ada", [1, 6 * D])
    norm1_g = din("norm1_g", [1, D])
    w_in = din("w_in", [D, 3488])
    q_latent_g = din("q_latent_g", [1, 256])
    w_uq = din("w_uq", [256, 768])
    kv_latent_g = din("kv_latent_g", [1, 128])
    w_ukv = din("w_ukv", [128, 1024])
    qk_q_g = din("qk_norm_q_g", [1, 96])
    qk_k_g = din("qk_norm_k_g", [1, 96])
    w_o_mla = din("w_o_mla", [512, D])
    conv_w = din("conv_w", [31, 512])
    conv_b = din("conv_b", [1, 512])
    conv_ln_g = din("conv_ln_g", [1, 512])
    conv_ln_b = din("conv_ln_b", [1, 512])
    w_pw_out = din("w_pw_out", [512, D])
    w_out = din("w_out", [D, D])
    norm2_g = din("norm2_g", [1, D])
    w_ff1 = din("w_ff1", [D, 4 * D])
    w_ff2 = din("w_ff2", [4 * D, D])
    cos_d = din("rope_cos", [S, 16])
    sin_d = din("rope_sin", [S, 16])
    ident_d = din("ident", [128, 128])
    wsrc = dict(w_in=w_in, w_pw_out=w_pw_out, w_out=w_out, w_ff1=w_ff1, w_ff2=w_ff2)

    out_d = nc.dram_tensor("out", [BPC, S, D], F32, kind="ExternalOutput").ap()
    panels = panel_list()
    NP = len(panels)
    wb_d = nc.dram_tensor("wb_scratch", [NP, 128, PANEL], BF16).ap()
    mod_d = nc.dram_tensor("mod_scratch", [BPC, 6 * D], F32).ap()

    st = contextlib.ExitStack()

    def sb(name, shape, dt):
        return st.enter_context(nc.sbuf_tensor(name, list(shape), dt))

    def pst(name):
        return st.enter_context(nc.psum_tensor(name, [128, 512], F32))

    def AP(t, off, dims):
        return bass.AP(t, off, [list(d) for d in dims])

    with st:
        ring = [sb(f"ring{i}", [128, PANEL], BF16) for i in range(RING)]
        xt = sb("xt", [128, 4, D], F32)
        xs = sb("xs", [128, D], BF16)
        junk = sb("junk", [128, D], BF16)
        h = sb("h", [128, 8, T], BF16)
        zlat = sb("zlat", [128, 416], F32)
        zn = sb("zn", [128, 384], BF16)
        znT = sb("znT", [128, 3, T], BF16)
        q = sb("q", [128, 768], F32)
        sq = sb("sq", [128, 768], F32)
        qb = sb("qb", [128, 768], BF16)
        qm = sb("qm", [128, 8, T], BF16)
        kb = sb("kb", [128, 768], BF16)
        Kc = sb("Kc", [96, NH, S], BF16)
        Vc = sb("Vc", [128, 16, 512], BF16)
        kscale = sb("kscale", [128, 16, NH], F32)
        PT = [sb(f"PT{i}", [128, T], BF16) for i in range(4)]
        PD = [sb(f"PD{i}", [128, T], BF16) for i in range(4)]
        attnT = sb("attnT", [64, NH, T], BF16)
        recip = sb("recip", [64, T], F32)
        tt = sb("tt", [128, T], F32)
        uT = sb("uT", [128, 4, 30 + T], BF16)
        cx = sb("cx", [128, T], F32)
        xn = sb("xn", [128, T], BF16)
        yy = sb("yy", [128, T], F32)
        t2 = sb("t2", [128, T], F32)
        u2T = sb("u2T", [128, 4, T], BF16)
        tg = sb("tg", [128, 16, T], BF16)
        tmpA = sb("tmpA", [128, T], F32)
        tmpB = sb("tmpB", [128, T], F32)
        gate1h = sb("gate1h", [128, D], F32)
        gate2b = sb("gate2b", [128, D], F32)
        convb_bc = sb("convb_bc", [128, 512], F32)
        gqlat_bc = sb("gqlat_bc", [128, 256], F32)
        gkvlat_bc = sb("gkvlat_bc", [128, 128], F32)
        gq_bc = sb("gq_bc", [128, 96], F32)
        gk_bc = sb("gk_bc", [128, 96], F32)
        cos_t = sb("cos_t", [128, 16, 16], F32)
        sin_t = sb("sin_t", [128, 16, 16], F32)
        ident_f = sb("ident_f", [128, 128], F32)
        ident = sb("ident_b", [128, 128], BF16)
        ones64 = sb("ones64_b", [128, 64], BF16)
        g1v = sb("g1v", [128, 8], F32)
        g2v = sb("g2v", [128, 8], F32)
        sc1 = sb("sc1", [128, 8], F32)
        sh1 = sb("sh1", [128, 8], F32)
        sc2 = sb("sc2", [128, 8], F32)
        sh2 = sb("sh2", [128, 8], F32)
        a1 = sb("a1", [128, 8], F32)
        a2 = sb("a2", [128, 8], F32)
        lng = sb("lng", [128, 4], F32)
        lnb = sb("lnb", [128, 4], F32)
        cwT = sb("cwT", [128, 4, 31], F32)
        ss = sb("ss", [128, 8], F32)
        rstd = sb("rstd", [128, 8], F32)
        st1 = sb("st1", [128, 4], F32)
        st2 = sb("st2", [128, 4], F32)
        ssq = sb("ssq", [128, 8], F32)
        r2 = sb("r2", [128, 8], F32)
        ssk = sb("ssk", [128, 8], F32)
        lnst = sb("lnst", [128, 4], F32)
        cmh = sb("cmh", [128, 1], F32)
        kr = sb("kr", [128, 32], F32)
        krb = sb("krb", [128, 32], BF16)
        rt = sb("rt", [128, 8, 16], F32)
        rt2 = sb("rt2", [128, 8, 16], F32)
        cT = sb("cT", [128, 8, BPC], F32)
        cT2 = sb("cT2", [128, 8, BPC], F32)
        modrow = sb("modrow", [BPC, 512], F32)
        bada = sb("bada", [BPC, 512], F32)
        ps = [pst(f"ps{i}") for i in range(8)]

        def psb(i):
            return ps[i].bitcast(BF16)

        def mm(out, lhsT, rhs, start, stop, reads, writes):
            P.op("pe", lambda e: e.matmul(out, lhsT=lhsT, rhs=rhs, start=start, stop=stop), reads=reads, writes=writes)

        def tr(out, in_, idn, reads, writes):
            P.op("pe", lambda e: e.transpose(out, in_, idn), reads=reads, writes=writes)

        def act(out, in_, func, reads, writes, scale=1.0, bias=0.0, accum=None):
            if accum is None:
                P.op("act", lambda e: e.activation(out=out, in_=in_, func=func, bias=bias, scale=scale), reads=reads, writes=writes)
            else:
                P.op("act", lambda e: e.activation(out=out, in_=in_, func=func, bias=bias, scale=scale, accum_out=accum), reads=reads, writes=writes)

        def tt_(q_, out, in0, in1, op, reads, writes):
            P.op(q_, lambda e: e.tensor_tensor(out=out, in0=in0, in1=in1, op=op), reads=reads, writes=writes)

        def ts_(q_, out, in0, s1, s2, op0, op1, reads, writes):
            if op1 is None:
                P.op(q_, lambda e: e.tensor_scalar(out=out, in0=in0, scalar1=s1, scalar2=None, op0=op0), reads=reads, writes=writes)
            else:
                P.op(q_, lambda e: e.tensor_scalar(out=out, in0=in0, scalar1=s1, scalar2=s2, op0=op0, op1=op1), reads=reads, writes=writes)

        def stt(out, in0, scalar, in1, op0, op1, reads, writes, accum=None):
            if accum is None:
                P.op("dve", lambda e: e.scalar_tensor_tensor(out=out, in0=in0, scalar=scalar, in1=in1, op0=op0, op1=op1), reads=reads, writes=writes)
            else:
                P.op("dve", lambda e: e.scalar_tensor_tensor(out=out, in0=in0, scalar=scalar, in1=in1, op0=op0, op1=op1, accum_out=accum), reads=reads, writes=writes)

        def cp(q_, out, in_, reads, writes):
            if q_ == "act":
                P.op(q_, lambda e: e.activation(out=out, in_=in_, func=AF.Identity), reads=reads, writes=writes)
            else:
                P.op(q_, lambda e: e.tensor_copy(out=out, in_=in_), reads=reads, writes=writes)

        def rsqrt_pool(out, in_, mult, reads, writes, post=None):
            ts_("pool", out, in_, mult, EPS, ALU.mult, ALU.add, reads, writes)
            n = out.shape[-1] if len(out.shape) > 1 else 1
            tt_("pool", out, out, cmh[:, 0:1].to_broadcast(list(out.shape)), ALU.pow, list(writes) + ["cmh"], writes)
            if post is not None:
                ts_("pool", out, out, post, 1.0, ALU.mult, ALU.mult, writes, writes)

        sload_i = [0]

        def sload(out, in_, writes, q_="sp", reads=()):
            sload_i[0] += 1
            P.dma(q_, f"sl{sload_i[0] % 4}", lambda e: e.dma_start(out=out, in_=in_, allow_slow_non_contiguous=True), reads=list(reads), writes=writes)

        P.op("pool", lambda e: e.memset(cmh[:], -0.5), writes=["cmh"])
        P.op("pool", lambda e: e.memset(ones64[:], 1.0), writes=["ones64"])
        for i in range(4):
            P.op("pool", lambda e, i=i: e.memset(PD[i][:], 0.0), writes=[f"PD{i}"])
        P.op("pool", lambda e: e.memset(uT[:], 0.0), writes=["uT"])
        for i in range(RING):
            P.op("pool", lambda e, i=i: e.memset(ring[i][:, :], 0.0), writes=[f"ring{i}"])
        sload(ident_f[:], ident_d, ["ident_f"])
        cp("dve", ident[:], ident_f[:], ["ident_f"], ["ident"])
        sload(cos_t[:], cos_d.rearrange("(b p) j -> p b j", p=128), ["cos"])
        sload(sin_t[:], sin_d.rearrange("(b p) j -> p b j", p=128), ["sin"])
        sload(g1v[:], norm1_g.rearrange("o (kc p) -> p (o kc)", p=128), ["g1v"])
        sload(g2v[:], norm2_g.rearrange("o (kc p) -> p (o kc)", p=128), ["g2v"])
        sload(lng[:], conv_ln_g.rearrange("o (kc p) -> p (o kc)", p=128), ["lng"])
        sload(lnb[:], conv_ln_b.rearrange("o (kc p) -> p (o kc)", p=128), ["lnb"])
        for cc_ in range(4):
            sload(cwT[:, cc_, :], conv_w[:, cc_ * 128:(cc_ + 1) * 128].rearrange("j p -> p j"), ["cwT"])
        sload(convb_bc[:], conv_b.partition_broadcast(128) if False else AP(conv_b.tensor, 0, [[0, 128], [1, 512]]), ["convb_bc"])
        sload(gqlat_bc[:], AP(q_latent_g.tensor, 0, [[0, 128], [1, 256]]), ["gqlat_bc"])
        sload(gkvlat_bc[:], AP(kv_latent_g.tensor, 0, [[0, 128], [1, 128]]), ["gkvlat_bc"])
        sload(gq_bc[:], AP(qk_q_g.tensor, 0, [[0, 128], [1, 96]]), ["gq_bc"])
        sload(gk_bc[:], AP(qk_k_g.tensor, 0, [[0, 128], [1, 96]]), ["gk_bc"])
        for b_ in range(BPC):
            sload(cT[:, :, b_], c_d[b_, :].rearrange("(kc p) -> p kc", p=128), ["cT"])

        act(cT2[:], cT[:], AF.Tanh, ["cT"], ["cT2"], scale=0.5)
        stt(cT2[:], cT2[:], 1.0, cT[:], ALU.add, ALU.mult, ["cT2", "cT"], ["cT2"])
        ts_("dve", cT2[:], cT2[:], 0.5, None, ALU.mult, None, ["cT2"], ["cT2"])
        stage = tg.reshape([128, 8192]).bitcast(F32)
        for g in range(12):
            P.dma("sp", "ada", lambda e, g=g: e.dma_start(
                out=stage[:, :].rearrange("p (kc c) -> p kc c", kc=8),
                in_=w_ada[:, g * 512:(g + 1) * 512].rearrange("(kc p) c -> p kc c", p=128)),
                reads=[], writes=["tg"])
            for kc in range(8):
                mm(ps[0][0:BPC, :], cT2[:, kc, :], stage[:, kc * 512:(kc + 1) * 512], kc == 0, kc == 7,
                   ["cT2", "tg"], ["ps0"])
            sload(bada[:], AP(b_ada.tensor, g * 512, [[0, BPC], [1, 512]]), ["bada"])
            tt_("dve", modrow[:], ps[0][0:BPC, :], bada[:], ALU.add, ["ps0", "bada"], ["modrow"])
            P.dma("sp", "modw", lambda e, g=g: e.dma_start(out=mod_d[:, g * 512:(g + 1) * 512], in_=modrow[:]),
                  reads=["modrow"], writes=["mod_d"])

        def fill_panel(pi, slot):
            name, kind, a = panels[pi]
            rk = f"ring{slot}"
            dst = ring[slot]
            if kind == "cols":
                src, nkc, c0, cw = a
                P.dma("pool", f"cv{slot}", lambda e: e.dma_start(
                    out=dst[:, 0:nkc * cw].rearrange("p (kc c) -> p kc c", kc=nkc),
                    in_=wsrc[src][:, c0:c0 + cw].rearrange("(kc p) c -> p kc c", p=128)),
                    reads=[], writes=[rk])
            elif kind == "rows":
                src, k0, nk, cw = a
                P.dma("pool", f"cv{slot}", lambda e: e.dma_start(
                    out=dst[:, 0:nk * cw].rearrange("p (kc c) -> p kc c", kc=nk),
                    in_=wsrc[src][k0 * 128:(k0 + nk) * 128, :].rearrange("(kc p) c -> p kc c", p=128)),
                    reads=[], writes=[rk])
            elif kind == "uqkv":
                P.dma("pool", f"cv{slot}", lambda e: e.dma_start(
                    out=dst[:, 0:1536].rearrange("p (kc c) -> p kc c", kc=2),
                    in_=w_uq.rearrange("(kc p) c -> p kc c", p=128)), reads=[], writes=[rk])
                P.dma("pool", f"cv{slot}", lambda e: e.dma_start(out=dst[:, 1536:2560], in_=w_ukv), reads=[], writes=[rk])
            elif kind == "wo":
                hf = a
                P.dma("pool", f"cv{slot}", lambda e: e.dma_start(
                    out=dst[0:64, 0:4096].rearrange("p (hh c) -> p hh c", hh=8),
                    in_=w_o_mla[:, hf * 512:(hf + 1) * 512].rearrange("(hh p) c -> p hh c", p=64)),
                    reads=[], writes=[rk])
            elif kind == "cdiag":
                cc = a
                for j in range(31):
                    ts_("dve" if j % 2 == 0 else "pool", dst[:, j * 128:(j + 1) * 128], ident_f[:], cwT[:, cc, j:j + 1], 1.0,
                        ALU.mult, ALU.mult, ["ident_f", "cwT"], [rk])
            P.dma("sp", f"wbw{slot}", lambda e: e.dma_start(out=wb_d[pi], in_=dst[:, :]), reads=[rk], writes=[f"wb{pi}"])

        for pi in range(NP):
            fill_panel(pi, pi % RING)

        gp = [0]
        cur = {}

        def load_next_panel():
            g = gp[0]
            slot = g % RING
            pi = g % NP
            P.dma("sp", f"rl{slot}", lambda e: e.dma_start(out=ring[slot][:, :], in_=wb_d[pi]), reads=[f"wb{pi}"], writes=[f"ring{slot}"])
            gp[0] += 1
            return slot

        pend = []

        def prefetch(n):
            while len(pend) < n:
                pend.append(load_next_panel())

        def take():
            prefetch(1)
            s_ = pend.pop(0)
            return s_

        out_ops = []
        bank_rr = [0]

        def nb4():
            b = bank_rr[0] % 4
            bank_rr[0] += 1
            return b

        def norm_to_h(gv_a, shv, tagks):
            for tb in range(4):
                act(junk[:], xt[:, tb, :], AF.Square, ["xt"], ["junk", "ss"], accum=ss[:, tb:tb + 1])
                rsqrt_pool(rstd[:, tb:tb + 1], ss[:, tb:tb + 1], 1.0 / D, ["ss"], ["rstd"])
                ts_("dve", xs[:], xt[:, tb, :], rstd[:, tb:tb + 1], None, ALU.mult, None, ["xt", "rstd"], ["xs"])
                b = nb4()
                pv = psb(b)
                for kc in range(8):
                    tr(pv[:, kc * 128:(kc + 1) * 128], xs[:, kc * 128:(kc + 1) * 128], ident[:], ["xs", "ident"], [f"ps{b}"])
                for kc in range(8):
                    act(h[:, kc, tb * 128:(tb + 1) * 128], pv[:, kc * 128:(kc + 1) * 128], AF.Identity,
                        [f"ps{b}"] + tagks, ["h"], scale=gv_a[:, kc:kc + 1], bias=shv[:, kc:kc + 1])

        for bi in range(BPC):
            def mrow(k):
                return mod_d[bi, k * D:(k + 1) * D]
            sload(sh1[:], mrow(0).rearrange("(kc p) -> p kc", p=128), ["sh1"], reads=["mod_d"])
            sload(sc1[:], mrow(1).rearrange("(kc p) -> p kc", p=128), ["sc1"], reads=["mod_d"])
            sload(sh2[:], mrow(3).rearrange("(kc p) -> p kc", p=128), ["sh2"], reads=["mod_d"])
            sload(sc2[:], mrow(4).rearrange("(kc p) -> p kc", p=128), ["sc2"], reads=["mod_d"])
            P.dma("sp", "g1", lambda e, bi=bi: e.dma_start(out=gate1h[:], in_=AP(mod_d.tensor, bi * 6 * D + 2 * D, [[0, 128], [1, D]])),
                  reads=["mod_d"], writes=["gate1h"])
            P.dma("sp", "g2", lambda e, bi=bi: e.dma_start(out=gate2b[:], in_=AP(mod_d.tensor, bi * 6 * D + 5 * D, [[0, 128], [1, D]])),
                  reads=["mod_d"], writes=["gate2b"])
            ts_("dve", gate1h[:], gate1h[:], 0.5, None, ALU.mult, None, ["gate1h"], ["gate1h"])
            stt(a1[:], sc1[:], 1.0, g1v[:], ALU.add, ALU.mult, ["sc1", "g1v"], ["a1"])
            stt(a2[:], sc2[:], 1.0, g2v[:], ALU.add, ALU.mult, ["sc2", "g2v"], ["a2"])

            for ti in range(NT):
                t0 = ti * T
                P.dma("sp", "xl", lambda e, bi=bi, t0=t0: e.dma_start(
                    out=xt[:], in_=x_d[bi, t0:t0 + T, :].rearrange("(tb p) d -> p tb d", p=128)),
                    reads=[], writes=["xt"])
                prefetch(2)
                norm_to_h(a1, sh1, ["a1", "sh1"])
                s_lat = take()
                s_uq = take()
                lat = ring[s_lat]
                uqp = ring[s_uq]
                for tb in range(4):
                    blk = ti * 4 + tb
                    tsl = slice(tb * 128, (tb + 1) * 128)
                    b = nb4()
                    for kc in range(8):
                        mm(ps[b][:, 0:416], h[:, kc, tsl], lat[:, kc * 416:(kc + 1) * 416], kc == 0, kc == 7,
                           ["h", f"ring{s_lat}"], [f"ps{b}"])
                    cp("dve", zlat[:], ps[b][:, 0:416], [f"ps{b}"], ["zlat"])
                    act(junk[:, 0:256], zlat[:, 0:256], AF.Square, ["zlat"], ["junk", "st1"], accum=st1[:, 0:1])
                    act(junk[:, 256:384], zlat[:, 256:384], AF.Square, ["zlat"], ["junk", "st1"], accum=st1[:, 1:2])
                    act(junk[:, 384:416], zlat[:, 384:416], AF.Square, ["zlat"], ["junk", "st1"], accum=st1[:, 2:3])
                    rsqrt_pool(st2[:, 0:1], st1[:, 0:1], 1.0 / 256, ["st1"], ["st2"])
                    rsqrt_pool(st2[:, 1:2], st1[:, 1:2], 1.0 / 128, ["st1"], ["st2"])
                    stt(zn[:, 0:256], zlat[:, 0:256], st2[:, 0:1], gqlat_bc[:], ALU.mult, ALU.mult, ["zlat", "st2", "gqlat_bc"], ["zn"])
                    stt(zn[:, 256:384], zlat[:, 256:384], st2[:, 1:2], gkvlat_bc[:], ALU.mult, ALU.mult, ["zlat", "st2", "gkvlat_bc"], ["zn"])
                    b = nb4()
                    pv = psb(b)
                    for c3 in range(3):
                        tr(pv[:, c3 * 128:(c3 + 1) * 128], zn[:, c3 * 128:(c3 + 1) * 128], ident[:], ["zn", "ident"], [f"ps{b}"])
                    cp("dve", znT[:, :, tsl], pv[:, 0:384].rearrange("p (c t) -> p c t", c=3), [f"ps{b}"], ["znT"])
                    bq = [nb4(), nb4()]
                    for hf in range(2):
                        for kc in range(2):
                            mm(ps[bq[hf]][:, 0:384], znT[:, kc, tsl], uqp[:, kc * 768 + hf * 384: kc * 768 + hf * 384 + 384],
                               kc == 0, kc == 1, ["znT", f"ring{s_uq}"], [f"ps{bq[hf]}"])
                    for hf in range(2):
                        act(sq[:, hf * 384:(hf + 1) * 384], ps[bq[hf]][:, 0:384], AF.Square, [f"ps{bq[hf]}"], ["sq"])
                    P.op("dve", lambda e: e.tensor_reduce(out=ssq[:], in_=sq[:].rearrange("p (h d) -> p h d", h=8), axis=AX.X, op=ALU.add),
                         reads=["sq"], writes=["ssq"])
                    rsqrt_pool(r2[:], ssq[:], 1.0 / 96, ["ssq"], ["r2"])
                    for hf in range(2):
                        tt_("dve", q[:, hf * 384:(hf + 1) * 384].rearrange("p (h d) -> p h d", h=4),
                            ps[bq[hf]][:, 0:384].rearrange("p (h d) -> p h d", h=4),
                            AP(r2, hf * 4, [[8, 128], [1, 4], [0, 96]]), ALU.mult, [f"ps{bq[hf]}", "r2"], ["q"])
                    q3 = q[:].rearrange("p (h d) -> p h d", h=8)
                    tt_("dve", q3, q3, AP(gq_bc, 0, [[96, 128], [0, 8], [1, 96]]), ALU.mult, ["q", "gq_bc"], ["q"])
                    qb3 = qb[:].rearrange("p (h d) -> p h d", h=8)
                    cosb = AP(cos_t, blk * 16, [[256, 128], [0, 8], [1, 16]])
                    sinb = AP(sin_t, blk * 16, [[256, 128], [0, 8], [1, 16]])
                    cp("pool", qb3[:, :, 0:64], q3[:, :, 0:64], ["q"], ["qb"])
                    tt_("dve", rt[:], q3[:, :, 64:80], cosb, ALU.mult, ["q", "cos"], ["rt"])
                    tt_("dve", rt2[:], q3[:, :, 80:96], sinb, ALU.mult, ["q", "sin"], ["rt2"])
                    tt_("dve", qb3[:, :, 64:80], rt[:], rt2[:], ALU.subtract, ["rt", "rt2"], ["qb"])
                    tt_("dve", rt[:], q3[:, :, 80:96], cosb, ALU.mult, ["q", "cos"], ["rt"])
                    tt_("dve", rt2[:], q3[:, :, 64:80], sinb, ALU.mult, ["q", "sin"], ["rt2"])
                    tt_("dve", qb3[:, :, 80:96], rt[:], rt2[:], ALU.add, ["rt", "rt2"], ["qb"])
                    b = nb4()
                    pv = psb(b)
                    for hh in range(8):
                        tr(pv[0:96, hh * 128:(hh + 1) * 128], qb[:, hh * 96:(hh + 1) * 96], ident[:], ["qb", "ident"], [f"ps{b}"])
                    cp("dve", qm[0:96, :, tsl], pv[0:96, :].rearrange("p (h t) -> p h t", h=8), [f"ps{b}"], ["qm"])
                    bk = [nb4(), nb4()]
                    for hf in range(2):
                        mm(ps[bk[hf]][:, :], znT[:, 2, tsl], uqp[:, 1536 + hf * 512:1536 + (hf + 1) * 512], True, True,
                           ["znT", f"ring{s_uq}"], [f"ps{bk[hf]}"])
                    for hf in range(2):
                        kv3 = ps[bk[hf]][:, :].rearrange("p (h d) -> p h d", h=4)
                        act(sq[:, hf * 256:(hf + 1) * 256].rearrange("p (h d) -> p h d", h=4), kv3[:, :, 0:64], AF.Square,
                            [f"ps{bk[hf]}"], ["sq"])
                    P.op("dve", lambda e: e.tensor_reduce(out=ssk[:], in_=sq[:, 0:512].rearrange("p (h d) -> p h d", h=8), axis=AX.X, op=ALU.add),
                         reads=["sq"], writes=["ssk"])
                    ts_("pool", ssk[:], ssk[:], st1[:, 2:3], 1.0, ALU.add, ALU.mult, ["ssk", "st1"], ["ssk"])
                    rsqrt_pool(kscale[:, blk, :], ssk[:], 1.0 / 96, ["ssk"], ["kscale"], post=96 ** -0.5)
                    kb3 = kb[:].rearrange("p (h d) -> p h d", h=8)
                    for hf in range(2):
                        kv3 = ps[bk[hf]][:, :].rearrange("p (h d) -> p h d", h=4)
                        tt_("dve", kb3[:, hf * 4:(hf + 1) * 4, 0:64], kv3[:, :, 0:64], AP(gk_bc, 0, [[96, 128], [0, 4], [1, 64]]), ALU.mult,
                            [f"ps{bk[hf]}", "gk_bc"], ["kb"])
                        cp("act", Vc[:, blk, hf * 256:(hf + 1) * 256].rearrange("p (h d) -> p h d", h=4), kv3[:, :, 64:128],
                           [f"ps{bk[hf]}"], ["Vc"])
                    tt_("dve", kr[:], zlat[:, 384:416], gk_bc[:, 64:96], ALU.mult, ["zlat", "gk_bc"], ["kr"])
                    c2 = cos_t[:, blk, :]
                    s2 = sin_t[:, blk, :]
                    tt_("dve", rt[:, 0, :], kr[:, 0:16], c2, ALU.mult, ["kr", "cos"], ["rt"])
                    tt_("dve", rt2[:, 0, :], kr[:, 16:32], s2, ALU.mult, ["kr", "sin"], ["rt2"])
                    tt_("dve", krb[:, 0:16], rt[:, 0, :], rt2[:, 0, :], ALU.subtract, ["rt", "rt2"], ["krb"])
                    tt_("dve", rt[:, 0, :], kr[:, 16:32], c2, ALU.mult, ["kr", "cos"], ["rt"])
                    tt_("dve", rt2[:, 0, :], kr[:, 0:16], s2, ALU.mult, ["kr", "sin"], ["rt2"])
                    tt_("dve", krb[:, 16:32], rt[:, 0, :], rt2[:, 0, :], ALU.add, ["rt", "rt2"], ["krb"])
                    cp("pool", kb3[:, :, 64:96], AP(krb, 0, [[32, 128], [0, 8], [1, 32]]), ["krb"], ["kb"])
                    b = nb4()
                    pv = psb(b)
                    for hh in range(8):
                        tr(pv[0:96, hh * 128:(hh + 1) * 128], kb[:, hh * 96:(hh + 1) * 96], ident[:], ["kb", "ident"], [f"ps{b}"])
                    cp("dve", Kc[:, :, blk * 128:(blk + 1) * 128], pv[0:96, :].rearrange("p (h t) -> p h t", h=8), [f"ps{b}"], ["Kc"])
                prefetch(2)
                s_a = take()
                s_b = take()
                for c4 in range(4):
                    ba, bb = nb4(), nb4()
                    for kc in range(8):
                        mm(ps[ba][:, :], ring[s_a][:, kc * 512 + c4 * 128: kc * 512 + c4 * 128 + 128], h[:, kc, :], kc == 0, kc == 7,
                           [f"ring{s_a}", "h"], [f"ps{ba}"])
                    for kc in range(8):
                        mm(ps[bb][:, :], ring[s_b][:, kc * 512 + c4 * 128: kc * 512 + c4 * 128 + 128], h[:, kc, :], kc == 0, kc == 7,
                           [f"ring{s_b}", "h"], [f"ps{bb}"])
                    act(tt[:], ps[bb][:, :], AF.Tanh, [f"ps{bb}"], ["tt"], scale=0.5)
                    stt(uT[:, c4, 30:30 + T], tt[:], 1.0, ps[ba][:, :], ALU.add, ALU.mult, ["tt", f"ps{ba}"], ["uT"])
                for g in range(4):
                    s_g = take()
                    prefetch(1)
                    for c4 in range(4):
                        b = nb4()
                        for kc in range(8):
                            mm(ps[b][:, :], ring[s_g][:, kc * 512 + c4 * 128: kc * 512 + c4 * 128 + 128], h[:, kc, :], kc == 0, kc == 7,
                               [f"ring{s_g}", "h"], [f"ps{b}"])
                        act(tg[:, g * 4 + c4, :], ps[b][:, :], AF.Tanh, [f"ps{b}"], ["tg"], scale=0.5)
                prefetch(2)
                nkb = 4 * ti + 4
                for hh in range(NH):
                    bacc = 4 + (hh % 2) * 2
                    bden = bacc + 1
                    for kbi in range(nkb):
                        j = kbi - 4 * ti
                        c0 = 128 * j if j >= 0 else 0
                        b = nb4()
                        mm(ps[b][:, c0:T], Kc[:, hh, kbi * 128:(kbi + 1) * 128], qm[0:96, hh, c0:T], True, True,
                           ["Kc", "qm"], [f"ps{b}"])
                        ksc = kscale[:, kbi, hh:hh + 1]
                        if j < 0:
                            pt = PT[kbi % 4]
                            ptk = f"PT{kbi % 4}"
                            act(pt[:, :], ps[b][:, :], AF.Exp, [f"ps{b}", "kscale"], [ptk], scale=ksc)
                        else:
                            pt = PD[j]
                            ptk = f"PD{j}"
                            if c0 + 64 < T:
                                act(pt[:, c0 + 64:T], ps[b][:, c0 + 64:T], AF.Exp, [f"ps{b}", "kscale"], [ptk], scale=ksc)
                            act(pt[0:64, c0:c0 + 64], ps[b][0:64, c0:c0 + 64], AF.Exp, [f"ps{b}", "kscale"], [ptk], scale=ksc[0:64, :])
                        mm(ps[bacc][0:64, c0:T], Vc[:, kbi, hh * 64:(hh + 1) * 64], pt[:, c0:T], kbi == 0, kbi == nkb - 1,
                           [ptk, "Vc"], [f"ps{bacc}"])
                        mm(ps[bden][0:64, c0:T], ones64[:, :], pt[:, c0:T], kbi == 0, kbi == nkb - 1,
                           [ptk, "ones64"], [f"ps{bden}"])
                    P.op("dve", lambda e, bden=bden: e.reciprocal(out=recip[:, :], in_=ps[bden][0:64, :]), reads=[f"ps{bden}"], writes=["recip"])
                    tt_("dve", attnT[:, hh, :], ps[bacc][0:64, :], recip[:, :], ALU.mult, [f"ps{bacc}", "recip"], ["attnT"])
                for cc in range(4):
                    s_c = take()
                    prefetch(1)
                    for tb in range(4):
                        for j in range(31):
                            mm(ps[4 + tb][:, cc * 128:(cc + 1) * 128], uT[:, cc, tb * 128 + j: tb * 128 + j + 128],
                               ring[s_c][:, j * 128:(j + 1) * 128], j == 0, j == 30, ["uT", f"ring{s_c}"], [f"ps{4 + tb}"])
                if ti == NT - 1:
                    P.op("pool", lambda e: e.memset(uT[:, :, 0:30], 0.0), reads=["uT"], writes=["uT"])
                else:
                    cp("pool", uT[:, :, 0:30], uT[:, :, T:T + 30], ["uT"], ["uT"])
                for tb in range(4):
                    tsl = slice(tb * 128, (tb + 1) * 128)
                    stt(cx[:], ps[4 + tb][:, :], 0.5, convb_bc[:], ALU.mult, ALU.add, [f"ps{4 + tb}", "convb_bc"], ["cx", "lnst"],
                        accum=lnst[:, 0:1])
                    act(junk[:, 0:512], cx[:], AF.Square, ["cx"], ["junk", "lnst"], accum=lnst[:, 1:2])
                    ts_("pool", lnst[:, 0:1], lnst[:, 0:1], 1.0 / 512, 1.0, ALU.mult, ALU.mult, ["lnst"], ["lnst"])
                    tt_("pool", lnst[:, 2:3], lnst[:, 0:1], lnst[:, 0:1], ALU.mult, ["lnst"], ["lnst"])
                    ts_("pool", lnst[:, 1:2], lnst[:, 1:2], 1.0 / 512, lnst[:, 2:3], ALU.mult, ALU.subtract, ["lnst"], ["lnst"])
                    rsqrt_pool(lnst[:, 1:2], lnst[:, 1:2], 1.0, ["lnst"], ["lnst"])
                    tt_("pool", lnst[:, 3:4], lnst[:, 0:1], lnst[:, 1:2], ALU.mult, ["lnst"], ["lnst"])
                    ts_("pool", lnst[:, 3:4], lnst[:, 3:4], -1.0, 1.0, ALU.mult, ALU.mult, ["lnst"], ["lnst"])
                    act(xn[:], cx[:], AF.Identity, ["cx", "lnst"], ["xn"], scale=lnst[:, 1:2], bias=lnst[:, 3:4])
                    b = nb4()
                    pv = psb(b)
                    for c4 in range(4):
                        tr(pv[:, c4 * 128:(c4 + 1) * 128], xn[:, c4 * 128:(c4 + 1) * 128], ident[:], ["xn", "ident"], [f"ps{b}"])
                    for c4 in range(4):
                        act(yy[:, c4 * 128:(c4 + 1) * 128], pv[:, c4 * 128:(c4 + 1) * 128], AF.Identity, [f"ps{b}", "lng", "lnb"], ["yy"],
                            scale=lng[:, c4:c4 + 1], bias=lnb[:, c4:c4 + 1])
                    act(t2[:], yy[:], AF.Tanh, ["yy"], ["t2"], scale=0.5)
                    stt(u2T[:, :, tsl], t2[:].rearrange("p (c t) -> p c t", c=4), 1.0, yy[:].rearrange("p (c t) -> p c t", c=4),
                        ALU.add, ALU.mult, ["t2", "yy"], ["u2T"])
                s_pw = take()
                s_o0 = take()
                s_o1 = take()
                for mc in range(8):
                    ba, bb = nb4(), nb4()
                    s_o = s_o0 if mc < 4 else s_o1
                    for hh in range(8):
                        mm(ps[ba][:, :], ring[s_o][0:64, hh * 512 + (mc % 4) * 128: hh * 512 + (mc % 4) * 128 + 128], attnT[:, hh, :],
                           hh == 0, hh == 7, [f"ring{s_o}", "attnT"], [f"ps{ba}"])
                    for kc in range(4):
                        mm(ps[bb][:, :], ring[s_pw][:, kc * 1024 + mc * 128: kc * 1024 + mc * 128 + 128], u2T[:, kc, :],
                           kc == 0, kc == 3, [f"ring{s_pw}", "u2T"], [f"ps{bb}"])
                    stt(tmpA[:], tg[:, mc, :], 1.0, ps[ba][:, :], ALU.add, ALU.mult, ["tg", f"ps{ba}"], ["tmpA"])
                    stt(tmpB[:], tg[:, 8 + mc, :], 1.0, ps[bb][:, :], ALU.add, ALU.mult, ["tg", f"ps{bb}"], ["tmpB"])
                    stt(qm[:, mc, :], tmpB[:], 0.5, tmpA[:], ALU.mult, ALU.add, ["tmpA", "tmpB"], ["qm"])
                prefetch(1)
                for hf in range(2):
                    s_w = take()
                    prefetch(1)
                    for tb in range(4):
                        b = nb4()
                        for kc in range(8):
                            mm(ps[b][:, :], qm[:, kc, tb * 128:(tb + 1) * 128], ring[s_w][:, kc * 512:(kc + 1) * 512], kc == 0, kc == 7,
                               ["qm", f"ring{s_w}"], [f"ps{b}"])
                        tt_("dve", tmpA[:], ps[b][:, :], gate1h[:, hf * 512:(hf + 1) * 512], ALU.mult, [f"ps{b}", "gate1h"], ["tmpA"])
                        tt_("dve", xt[:, tb, hf * 512:(hf + 1) * 512], xt[:, tb, hf * 512:(hf + 1) * 512], tmpA[:], ALU.add,
                            ["xt", "tmpA"], ["xt"])
                norm_to_h(a2, sh2, ["a2", "sh2"])
                for dh in range(2):
                    for p in range(4):
                        s_f = take()
                        prefetch(1)
                        for c4 in range(4):
                            b = nb4()
                            for kc in range(8):
                                mm(ps[b][:, :], ring[s_f][:, kc * 512 + c4 * 128: kc * 512 + c4 * 128 + 128], h[:, kc, :], kc == 0, kc == 7,
                                   [f"ring{s_f}", "h"], [f"ps{b}"])
                            act(tmpB[:], ps[b][:, :], AF.Relu, [f"ps{b}"], ["tmpB"])
                            tt_("dve", tg[:, p * 4 + c4, :], tmpB[:], tmpB[:], ALU.mult, ["tmpB"], ["tg"])
                    for p in range(4):
                        s_f = take()
                        prefetch(1)
                        for k4 in range(4):
                            kc = p * 4 + k4
                            for tb in range(4):
                                for hf in range(2):
                                    bo = tb * 2 + hf
                                    mm(ps[bo][:, :], tg[:, kc, tb * 128:(tb + 1) * 128],
                                       ring[s_f][:, k4 * 1024 + hf * 512: k4 * 1024 + hf * 512 + 512], kc == 0, kc == 15,
                                       ["tg", f"ring{s_f}"], [f"ps{bo}"])
                    for tb in range(4):
                        for hf in range(2):
                            bo = tb * 2 + hf
                            tt_("dve", tmpA[:], ps[bo][:, :], gate2b[:, hf * 512:(hf + 1) * 512], ALU.mult, [f"ps{bo}", "gate2b"], ["tmpA"])
                            tt_("dve", xt[:, tb, hf * 512:(hf + 1) * 512], xt[:, tb, hf * 512:(hf + 1) * 512], tmpA[:], ALU.add,
                                ["xt", "tmpA"], ["xt"])
                oi = P.dma("sp", "xo", lambda e, bi=bi, t0=t0: e.dma_start(
                    out=out_d[bi, t0:t0 + T, :].rearrange("(tb p) d -> p tb d", p=128), in_=xt[:]),
                    reads=["xt"], writes=["out_d"])
                out_ops.append(oi)

        P.emit(final_wait_ops=out_ops[-1:])
    return nc, P


def _rope_tables():
    inv_freq = (np.float32(10000.0) ** (-np.arange(0, 32, 2, dtype=np.float32) / np.float32(32))).astype(np.float32)
    ang = (np.arange(S, dtype=np.float32)[:, None] * inv_freq[None, :]).astype(np.float32)
    return np.cos(ang).astype(np.float32), np.sin(ang).astype(np.float32)


_CACHE = {}


def kernel(**inputs):
    if "nc" not in _CACHE:
        _CACHE["nc"] = build_nc()
    nc, _ = _CACHE["nc"]
    cos, sin = _rope_tables()
    ident = np.eye(128, dtype=np.float32)
    x = np.ascontiguousarray(np.asarray(inputs["x"], dtype=np.float32))
    c = np.ascontiguousarray(np.asarray(inputs["c"], dtype=np.float32))
    shared = {}
    for k, v in inputs.items():
        if k in ("x", "c"):
            continue
        a = np.asarray(v, dtype=np.float32)
        shared[k] = np.ascontiguousarray(a[0] if a.ndim == 3 else a)
    shared["rope_cos"] = cos
    shared["rope_sin"] = sin
    shared["ident"] = ident
    in_maps = []
    for i in range(NCORES):
        m = dict(shared)
        m["x"] = x[i * BPC:(i + 1) * BPC]
        m["c"] = c[i * BPC:(i + 1) * BPC]
        in_maps.append(m)
    res = run_bass_kernel_spmd(nc, in_maps, core_ids=list(range(NCORES)))
    return np.concatenate([r["out"] for r in res.results], axis=0)
```

```python
import contextlib
import numpy as np
import concourse.bass as bass
import concourse.mybir as mybir
from concourse.bass_utils import run_bass_kernel_spmd

F32 = mybir.dt.float32
BF16 = mybir.dt.bfloat16
AF = mybir.ActivationFunctionType
ALU = mybir.AluOpType
AX = mybir.AxisListType

COMPUTE = ("pe", "act", "dve", "pool")
QUEUES = ("pe", "act", "dve", "pool", "sp")

NCORES = 8
D = 1024
S = 2048
BPC = 4
T = 512
NT = S // T
NH = 8
EPS = 1e-6
PANEL = 4096
RING = 3


class Prog:
    def __init__(self, nc, epoch_len=2000):
        self.nc = nc
        self.ops = []
        self.per_q = {q: [] for q in QUEUES}
        self.last_w = {}
        self.readers = {}
        self.stream_last = {}
        self.stream_cnt = {}
        self.epoch_len = epoch_len
        self.phase = ""
        self.annotate = False

    def _deps(self, reads, writes, me):
        deps = set()
        for r in reads:
            w = self.last_w.get(r)
            if w is not None:
                deps.add(w)
        for w_ in writes:
            w = self.last_w.get(w_)
            if w is not None:
                deps.add(w)
            for rd in self.readers.get(w_, ()):
                deps.add(rd)
        deps.discard(me)
        return deps

    def _commit(self, reads, writes, me):
        mo = self.ops[me]
        for r in reads:
            lst = self.readers.setdefault(r, [])
            if mo["kind"] == "c":
                lst[:] = [x for x in lst if not (self.ops[x]["kind"] == "c" and self.ops[x]["q"] == mo["q"])]
            lst.append(me)
        for w_ in writes:
            self.last_w[w_] = me
            self.readers[w_] = []

    def op(self, q, fn, reads=(), writes=()):
        idx = len(self.ops)
        deps = self._deps(reads, writes, idx)
        if q == "pe":
            deps = {d for d in deps if not (self.ops[d]["kind"] == "c" and self.ops[d]["q"] == "pe")}
        self.ops.append(dict(kind="c", q=q, fn=fn, deps=deps, sig=False, ph=self.phase))
        self._commit(reads, writes, idx)
        return idx

    def dma(self, q, stream, fn, reads=(), writes=()):
        idx = len(self.ops)
        deps = self._deps(reads, writes, idx)
        prev = self.stream_last.get(stream)
        if prev is not None:
            deps.add(prev)
        n = self.stream_cnt.get(stream, 0) + 1
        self.stream_cnt[stream] = n
        self.stream_last[stream] = idx
        self.ops.append(dict(kind="d", q=q, fn=fn, deps=deps, stream=stream, val=16 * n, sig=True, ph=self.phase))
        self._commit(reads, writes, idx)
        return idx

    def emit(self, final_wait_ops=()):
        nc = self.nc
        ops = self.ops
        for o in ops:
            for d in o["deps"]:
                if ops[d]["kind"] == "c":
                    ops[d]["sig"] = True
        signum = {}
        cnt = {q: 0 for q in COMPUTE}
        for i, o in enumerate(ops):
            if o["kind"] == "c" and o["sig"]:
                cnt[o["q"]] += 1
                signum[i] = cnt[o["q"]]
        stack = contextlib.ExitStack()
        EL = self.epoch_len
        sems = {}
        for q in COMPUTE:
            ne = cnt[q] // EL + 1
            sems[q] = [stack.enter_context(nc.semaphore(f"s_{q}_{e}")) for e in range(ne)]
        ssem = {s: stack.enter_context(nc.semaphore(f"d_{s}")) for s in self.stream_cnt}
        clock = {q: {} for q in QUEUES}
        snap = {}
        plans = {q: [] for q in QUEUES}
        for i, o in enumerate(ops):
            q = o["q"]
            ck = clock[q]
            wm = {}
            for d in sorted(o["deps"]):
                od = ops[d]
                if od["kind"] == "c":
                    key, v = od["q"], signum[d]
                else:
                    key, v = "S:" + od["stream"], od["val"]
                if ck.get(key, 0) >= v:
                    continue
                wm[key] = max(wm.get(key, 0), v)
                for k2, v2 in snap[d].items():
                    if ck.get(k2, 0) < v2:
                        ck[k2] = v2
                if ck.get(key, 0) < v:
                    ck[key] = v
            s = dict(ck)
            if o["kind"] == "c":
                if o["sig"]:
                    s[q] = max(s.get(q, 0), signum[i])
            else:
                s["S:" + o["stream"]] = o["val"]
            snap[i] = s
            plans[q].append((i, wm))
        self.stats = dict(cnt=cnt, nops=len(ops), nsem=sum(len(v) for v in sems.values()) + len(ssem))

        def run(q, eng):
            for i, wm in plans[q]:
                o = ops[i]
                for k, v in wm.items():
                    if k.startswith("S:"):
                        eng.wait_ge(ssem[k[2:]], v)
                    else:
                        e = (v - 1) // EL
                        eng.wait_ge(sems[k][e], v - e * EL)
                ins = o["fn"](eng)
                if self.annotate:
                    ins.annotate(o["ph"])
                if o["kind"] == "d":
                    ins.then_inc(ssem[o["stream"]], 16)
                elif o["sig"]:
                    e = (signum[i] - 1) // EL
                    ins.then_inc(sems[q][e], 1)
            if q == "sp":
                for d in final_wait_ops:
                    od = ops[d]
                    eng.wait_ge(ssem[od["stream"]], od["val"])

        with stack:
            with nc.Block() as block:
                @block.tensor
                def _(e):
                    run("pe", e)

                @block.scalar
                def _(e):
                    run("act", e)

                @block.vector
                def _(e):
                    run("dve", e)

                @block.gpsimd
                def _(e):
                    run("pool", e)

                @block.sync
                def _(e):
                    run("sp", e)


def panel_list():
    pl = []
    pl.append(("lat", "cols", ("w_in", 8, 0, 416)))
    pl.append(("uqkv", "uqkv", None))
    pl.append(("gluB", "cols", ("w_in", 8, 928, 512)))
    pl.append(("gluA", "cols", ("w_in", 8, 416, 512)))
    for g in range(4):
        pl.append((f"gate{g}", "cols", ("w_in", 8, 1440 + 512 * g, 512)))
    for cc in range(4):
        pl.append((f"cdiag{cc}", "cdiag", cc))
    pl.append(("pw", "cols", ("w_pw_out", 4, 0, 1024)))
    pl.append(("wo", "cols", ("w_o_mla", 4, 0, 1024)))
    pl.append(("wout0", "cols", ("w_out", 8, 0, 512)))
    pl.append(("wout1", "cols", ("w_out", 8, 512, 512)))
    for dh in range(2):
        for p in range(4):
            pl.append((f"ff1_{dh}_{p}", "cols", ("w_ff1", 8, 2048 * dh + 512 * p, 512)))
        for p in range(4):
            pl.append((f"ff2_{dh}_{p}", "rows", ("w_ff2", 16 * dh + 4 * p, 4, 1024)))
    return pl


def build_nc(BPC=BPC, NT=NT, annotate=False):
    nc = bass.Bass("TRN2", target_bir_lowering=False)
    P = Prog(nc)
    P.annotate = annotate
    P.phase = "pro"

    def din(name, shape, dt=F32):
        return nc.dram_tensor(name, list(shape), dt, kind="ExternalInput").ap()

    x_d = din("x", [BPC, S, D])
    c_d = din("c", [BPC, D])
    w_ada = din("w_ada", [D, 6 * D])
    b_ada = din("b_ada", [1, 6 * D])
    norm1_g = din("norm1_g", [1, D])
    w_in = din("w_in", [D, 3488])
    q_latent_g = din("q_latent_g", [1, 256])
    w_uq = din("w_uq", [256, 768])
    kv_latent_g = din("kv_latent_g", [1, 128])
    w_ukv = din("w_ukv", [128, 1024])
    qk_q_g = din("qk_norm_q_g", [1, 96])
    qk_k_g = din("qk_norm_k_g", [1, 96])
    w_o_mla = din("w_o_mla", [512, D])
    conv_w = din("conv_w", [31, 512])
    conv_b = din("conv_b", [1, 512])
    conv_ln_g = din("conv_ln_g", [1, 512])
    conv_ln_b = din("conv_ln_b", [1, 512])
    w_pw_out = din("w_pw_out", [512, D])
    w_out = din("w_out", [D, D])
    norm2_g = din("norm2_g", [1, D])
    w_ff1 = din("w_ff1", [D, 4 * D])
    w_ff2 = din("w_ff2", [4 * D, D])
    cos_d = din("rope_cos", [S, 16])
    sin_d = din("rope_sin", [S, 16])
    ident_d = din("ident", [128, 128])
    wsrc = dict(w_in=w_in, w_pw_out=w_pw_out, w_out=w_out, w_ff1=w_ff1, w_ff2=w_ff2, w_o_mla=w_o_mla)

    out_d = nc.dram_tensor("out", [BPC, S, D], F32, kind="ExternalOutput").ap()
    panels = panel_list()
    NP = len(panels)
    wb_d = nc.dram_tensor("wb_scratch", [NP, 128, PANEL], BF16).ap()
    mod_d = nc.dram_tensor("mod_scratch", [BPC, 6 * D], F32).ap()

    st = contextlib.ExitStack()

    def sb(name, shape, dt):
        return st.enter_context(nc.sbuf_tensor(name, list(shape), dt))

    def pst(name):
        return st.enter_context(nc.psum_tensor(name, [128, 512], F32))

    def AP(t, off, dims):
        return bass.AP(t, off, [list(d) for d in dims])

    with st:
        ring = [sb(f"ring{i}", [128, PANEL], BF16) for i in range(RING)]
        xt = sb("xt", [128, 4, D], F32)
        xs2 = [sb(f"xs{i}", [128, D], BF16) for i in range(2)]
        junk2 = sb("junk2", [128, 512], BF16)
        junk = sb("junk", [128, D], BF16)
        h = sb("h", [128, 8, T], BF16)
        zlat = sb("zlat", [128, 416], F32)
        zn = sb("zn", [128, 384], BF16)
        znT = sb("znT", [128, 3, T], BF16)
        q = sb("q", [128, 768], F32)
        sq = sb("sq", [128, 768], F32)
        qb = sb("qb", [128, 768], BF16)
        qm = sb("qm", [128, 8, T], BF16)
        kb = sb("kb", [128, 768], BF16)
        Kc = sb("Kc", [96, NH, S], BF16)
        Vc = sb("Vc", [128, 16, 512], BF16)
        kscale = sb("kscale", [128, 16, NH], F32)
        PT = [sb(f"PT{i}", [128, T], BF16) for i in range(4)]
        PD = [sb(f"PD{i}", [128, T], BF16) for i in range(4)]
        attnT = sb("attnT", [128, 4, T], BF16)
        recip = sb("recip", [128, T], F32)
        tt = sb("tt", [128, T], F32)
        uT = sb("uT", [128, 4, 30 + T], BF16)
        cx4 = sb("cx4", [128, 4, T], F32)
        lnst4 = sb("lnst4", [128, 4, 4], F32)
        zkr = sb("zkr", [128, 4, 32], F32)
        sskr = sb("sskr", [128, 4], F32)
        xn = sb("xn", [128, T], BF16)
        yy = sb("yy", [128, T], F32)
        t2 = sb("t2", [128, T], F32)
        u2T = sb("u2T", [128, 4, T], BF16)
        tg = sb("tg", [128, 16, T], BF16)
        tmpA = sb("tmpA", [128, T], F32)
        tmpB = sb("tmpB", [128, T], F32)
        gate1h = sb("gate1h", [128, D], F32)
        gate2b = sb("gate2b", [128, D], F32)
        convb_bc = sb("convb_bc", [128, 512], F32)
        gqlat_bc = sb("gqlat_bc", [128, 256], F32)
        gkvlat_bc = sb("gkvlat_bc", [128, 128], F32)
        gq_bc = sb("gq_bc", [128, 96], F32)
        gk_bc = sb("gk_bc", [128, 96], F32)
        cos_t = sb("cos_t", [128, 16, 16], F32)
        sin_t = sb("sin_t", [128, 16, 16], F32)
        ident_f = sb("ident_f", [128, 128], F32)
        ident = sb("ident_b", [128, 128], BF16)
        ones64 = sb("ones64_b", [128, 64], BF16)
        g1v = sb("g1v", [128, 8], F32)
        g2v = sb("g2v", [128, 8], F32)
        sc1 = sb("sc1", [128, 8], F32)
        sh1 = sb("sh1", [128, 8], F32)
        sc2 = sb("sc2", [128, 8], F32)
        sh2 = sb("sh2", [128, 8], F32)
        a1 = sb("a1", [128, 8], F32)
        a2 = sb("a2", [128, 8], F32)
        lng = sb("lng", [128, 4], F32)
        lnb = sb("lnb", [128, 4], F32)
        cwT = sb("cwT", [128, 4, 31], F32)
        ss = sb("ss", [128, 8], F32)
        rstd = sb("rstd", [128, 8], F32)
        st1 = sb("st1", [128, 4], F32)
        st2 = sb("st2", [128, 4], F32)
        ssq = sb("ssq", [128, 8], F32)
        r2 = sb("r2", [128, 8], F32)
        ssk = sb("ssk", [128, 8], F32)
        cmh = sb("cmh", [128, 1], F32)
        kr = sb("kr", [128, 32], F32)
        krb = sb("krb", [128, 32], BF16)
        rt = sb("rt", [128, 8, 16], F32)
        rt2 = sb("rt2", [128, 8, 16], F32)
        cT = sb("cT", [128, 8, BPC], F32)
        cT2 = sb("cT2", [128, 8, BPC], F32)
        modrow = sb("modrow", [BPC, 512], F32)
        bada = sb("bada", [BPC, 512], F32)
        ps = [pst(f"ps{i}") for i in range(8)]

        def psb(i):
            return ps[i].bitcast(BF16)

        def mm(out, lhsT, rhs, start, stop, reads, writes):
            P.op("pe", lambda e: e.matmul(out, lhsT=lhsT, rhs=rhs, start=start, stop=stop), reads=reads, writes=writes)

        def tr(out, in_, idn, reads, writes):
            P.op("pe", lambda e: e.transpose(out, in_, idn), reads=reads, writes=writes)

        def act(out, in_, func, reads, writes, scale=1.0, bias=0.0, accum=None):
            if accum is None:
                P.op("act", lambda e: e.activation(out=out, in_=in_, func=func, bias=bias, scale=scale), reads=reads, writes=writes)
            else:
                P.op("act", lambda e: e.activation(out=out, in_=in_, func=func, bias=bias, scale=scale, accum_out=accum), reads=reads, writes=writes)

        def tt_(q_, out, in0, in1, op, reads, writes):
            P.op(q_, lambda e: e.tensor_tensor(out=out, in0=in0, in1=in1, op=op), reads=reads, writes=writes)

        def ts_(q_, out, in0, s1, s2, op0, op1, reads, writes):
            if op1 is None:
                P.op(q_, lambda e: e.tensor_scalar(out=out, in0=in0, scalar1=s1, scalar2=None, op0=op0), reads=reads, writes=writes)
            else:
                P.op(q_, lambda e: e.tensor_scalar(out=out, in0=in0, scalar1=s1, scalar2=s2, op0=op0, op1=op1), reads=reads, writes=writes)

        def stt(out, in0, scalar, in1, op0, op1, reads, writes, accum=None):
            if accum is None:
                P.op("dve", lambda e: e.scalar_tensor_tensor(out=out, in0=in0, scalar=scalar, in1=in1, op0=op0, op1=op1), reads=reads, writes=writes)
            else:
                P.op("dve", lambda e: e.scalar_tensor_tensor(out=out, in0=in0, scalar=scalar, in1=in1, op0=op0, op1=op1, accum_out=accum), reads=reads, writes=writes)

        def cp(q_, out, in_, reads, writes):
            if q_ == "act":
                P.op(q_, lambda e: e.activation(out=out, in_=in_, func=AF.Identity), reads=reads, writes=writes)
            else:
                P.op(q_, lambda e: e.tensor_copy(out=out, in_=in_), reads=reads, writes=writes)

        def rsqrt_pool(out, in_, mult, reads, writes, post=None):
            ts_("pool", out, in_, mult, EPS, ALU.mult, ALU.add, reads, writes)
            n = out.shape[-1] if len(out.shape) > 1 else 1
            tt_("pool", out, out, cmh[:, 0:1].to_broadcast(list(out.shape)), ALU.pow, list(writes) + ["cmh"], writes)
            if post is not None:
                ts_("pool", out, out, post, 1.0, ALU.mult, ALU.mult, writes, writes)

        sload_i = [0]

        def sload(out, in_, writes, q_="sp", reads=()):
            sload_i[0] += 1
            P.dma(q_, f"sl{sload_i[0] % 4}", lambda e: e.dma_start(out=out, in_=in_, allow_slow_non_contiguous=True), reads=list(reads), writes=writes)

        P.op("pool", lambda e: e.memset(cmh[:], -0.5), writes=["cmh"])
        P.op("pool", lambda e: e.memset(ones64[:], 1.0), writes=["ones64"])
        for i in range(4):
            P.op("pool", lambda e, i=i: e.memset(PD[i][:], 0.0), writes=[f"PD{i}"])
        P.op("pool", lambda e: e.memset(uT[:], 0.0), writes=["uT"])
        for i in range(RING):
            P.op("pool", lambda e, i=i: e.memset(ring[i][:, :], 0.0), writes=[f"ring{i}"])
        sload(ident_f[:], ident_d, ["ident_f"])
        cp("dve", ident[:], ident_f[:], ["ident_f"], ["ident"])
        sload(cos_t[:], cos_d.rearrange("(b p) j -> p b j", p=128), ["cos"])
        sload(sin_t[:], sin_d.rearrange("(b p) j -> p b j", p=128), ["sin"])
        sload(g1v[:], norm1_g.rearrange("o (kc p) -> p (o kc)", p=128), ["g1v"])
        sload(g2v[:], norm2_g.rearrange("o (kc p) -> p (o kc)", p=128), ["g2v"])
        sload(lng[:], conv_ln_g.rearrange("o (kc p) -> p (o kc)", p=128), ["lng"])
        sload(lnb[:], conv_ln_b.rearrange("o (kc p) -> p (o kc)", p=128), ["lnb"])
        for cc_ in range(4):
            sload(cwT[:, cc_, :], conv_w[:, cc_ * 128:(cc_ + 1) * 128].rearrange("j p -> p j"), ["cwT"])
        sload(convb_bc[:], conv_b.partition_broadcast(128) if False else AP(conv_b.tensor, 0, [[0, 128], [1, 512]]), ["convb_bc"])
        sload(gqlat_bc[:], AP(q_latent_g.tensor, 0, [[0, 128], [1, 256]]), ["gqlat_bc"])
        sload(gkvlat_bc[:], AP(kv_latent_g.tensor, 0, [[0, 128], [1, 128]]), ["gkvlat_bc"])
        sload(gq_bc[:], AP(qk_q_g.tensor, 0, [[0, 128], [1, 96]]), ["gq_bc"])
        sload(gk_bc[:], AP(qk_k_g.tensor, 0, [[0, 128], [1, 96]]), ["gk_bc"])
        for b_ in range(BPC):
            sload(cT[:, :, b_], c_d[b_, :].rearrange("(kc p) -> p kc", p=128), ["cT"])

        act(cT2[:], cT[:], AF.Tanh, ["cT"], ["cT2"], scale=0.5)
        stt(cT2[:], cT2[:], 1.0, cT[:], ALU.add, ALU.mult, ["cT2", "cT"], ["cT2"])
        ts_("dve", cT2[:], cT2[:], 0.5, None, ALU.mult, None, ["cT2"], ["cT2"])
        TGK = [f"tg{i}" for i in range(16)]
        stage = tg.reshape([128, 8192]).bitcast(F32)
        for g in range(12):
            P.dma("sp", "ada", lambda e, g=g: e.dma_start(
                out=stage[:, :].rearrange("p (kc c) -> p kc c", kc=8),
                in_=w_ada[:, g * 512:(g + 1) * 512].rearrange("(kc p) c -> p kc c", p=128)),
                reads=[], writes=TGK)
            for kc in range(8):
                mm(ps[0][0:BPC, :], cT2[:, kc, :], stage[:, kc * 512:(kc + 1) * 512], kc == 0, kc == 7,
                   ["cT2"] + TGK, ["ps0"])
            sload(bada[:], AP(b_ada.tensor, g * 512, [[0, BPC], [1, 512]]), ["bada"])
            tt_("dve", modrow[:], ps[0][0:BPC, :], bada[:], ALU.add, ["ps0", "bada"], ["modrow"])
            P.dma("sp", "modw", lambda e, g=g: e.dma_start(out=mod_d[:, g * 512:(g + 1) * 512], in_=modrow[:]),
                  reads=["modrow"], writes=["mod_d"])

        def fill_panel(pi, slot):
            name, kind, a = panels[pi]
            rk = f"ring{slot}"
            dst = ring[slot]
            if kind == "cols":
                src, nkc, c0, cw = a
                P.dma("pool", f"cv{slot}", lambda e: e.dma_start(
                    out=dst[:, 0:nkc * cw].rearrange("p (kc c) -> p kc c", kc=nkc),
                    in_=wsrc[src][:, c0:c0 + cw].rearrange("(kc p) c -> p kc c", p=128)),
                    reads=[], writes=[rk])
            elif kind == "rows":
                src, k0, nk, cw = a
                P.dma("pool", f"cv{slot}", lambda e: e.dma_start(
                    out=dst[:, 0:nk * cw].rearrange("p (kc c) -> p kc c", kc=nk),
                    in_=wsrc[src][k0 * 128:(k0 + nk) * 128, :].rearrange("(kc p) c -> p kc c", p=128)),
                    reads=[], writes=[rk])
            elif kind == "uqkv":
                P.dma("pool", f"cv{slot}", lambda e: e.dma_start(
                    out=dst[:, 0:1536].rearrange("p (kc c) -> p kc c", kc=2),
                    in_=w_uq.rearrange("(kc p) c -> p kc c", p=128)), reads=[], writes=[rk])
                P.dma("pool", f"cv{slot}", lambda e: e.dma_start(out=dst[:, 1536:2560], in_=w_ukv), reads=[], writes=[rk])
            elif kind == "wo":
                hf = a
                P.dma("pool", f"cv{slot}", lambda e: e.dma_start(
                    out=dst[0:64, 0:4096].rearrange("p (hh c) -> p hh c", hh=8),
                    in_=w_o_mla[:, hf * 512:(hf + 1) * 512].rearrange("(hh p) c -> p hh c", p=64)),
                    reads=[], writes=[rk])
            elif kind == "cdiag":
                cc = a
                for j in range(31):
                    ts_("dve" if j % 2 == 0 else "pool", dst[:, j * 128:(j + 1) * 128], ident_f[:], cwT[:, cc, j:j + 1], 1.0,
                        ALU.mult, ALU.mult, ["ident_f", "cwT"], [rk])
            P.dma("sp", f"wbw{slot}", lambda e: e.dma_start(out=wb_d[pi], in_=dst[:, :]), reads=[rk], writes=[f"wb{pi}"])

        for pi in range(NP):
            fill_panel(pi, pi % RING)

        pname = {nm: i for i, (nm, _, _) in enumerate(panels)}
        PR = [P]
        stt_ = dict(seq=None, pos=0, rec=[], pend=[], free=list(range(RING)))

        def auto_prefetch():
            if stt_["seq"] is None:
                return
            while stt_["free"] and len(stt_["pend"]) < 2:
                nm = stt_["seq"][stt_["pos"] % len(stt_["seq"])]
                stt_["pos"] += 1
                slot = stt_["free"].pop(0)
                pi = pname[nm]
                PR[0].dma("sp", f"rl{slot}", lambda e, slot=slot, pi=pi: e.dma_start(out=ring[slot][:, :], in_=wb_d[pi]),
                          reads=[f"wb{pi}"], writes=[f"ring{slot}"])
                stt_["pend"].append((nm, slot))

        def take(nm):
            if stt_["seq"] is None:
                stt_["rec"].append(nm)
                return 0
            auto_prefetch()
            assert stt_["pend"], "panel ring exhausted"
            n2, s_ = stt_["pend"].pop(0)
            assert n2 == nm, (n2, nm)
            return s_

        def release(s_):
            if stt_["seq"] is None:
                return
            stt_["free"].append(s_)
            auto_prefetch()

        out_ops = []

        def mk_alloc(banks):
            st_ = [0]

            def f():
                b = banks[st_[0] % len(banks)]
                st_[0] += 1
                return b
            return f

        nb4 = mk_alloc([0, 1, 2, 3])
        nb8 = mk_alloc([0, 1, 2, 3, 4, 5, 6, 7])
        nbA = mk_alloc([0, 1])
        nbAt = mk_alloc([2])
        nbB = mk_alloc([3, 4, 5, 6, 7])
        nbS = mk_alloc([0, 1, 2])
        nbL = mk_alloc([3])

        def pk(b):
            return [f"ps{b}"] if b < 4 else [f"ps{b}", f"ps{b}h0", f"ps{b}h64"]

        def hk(b, po):
            return f"ps{b}h{po}"

        def op(q_, fn, reads=(), writes=()):
            return PR[0].op(q_, fn, reads=reads, writes=writes)

        def mm(out, lhsT, rhs, start, stop, reads, writes):
            op("pe", lambda e: e.matmul(out, lhsT=lhsT, rhs=rhs, start=start, stop=stop), reads, writes)

        def tr(out, in_, idn, reads, writes):
            op("pe", lambda e: e.transpose(out, in_, idn), reads, writes)

        def act(out, in_, func, reads, writes, scale=1.0, bias=0.0, accum=None):
            if accum is None:
                op("act", lambda e: e.activation(out=out, in_=in_, func=func, bias=bias, scale=scale), reads, writes)
            else:
                op("act", lambda e: e.activation(out=out, in_=in_, func=func, bias=bias, scale=scale, accum_out=accum), reads, writes)

        def tt_(q_, out, in0, in1, op_, reads, writes):
            op(q_, lambda e: e.tensor_tensor(out=out, in0=in0, in1=in1, op=op_), reads, writes)

        def ts_(q_, out, in0, s1, s2, op0, op1, reads, writes):
            if op1 is None:
                op(q_, lambda e: e.tensor_scalar(out=out, in0=in0, scalar1=s1, scalar2=None, op0=op0), reads, writes)
            else:
                op(q_, lambda e: e.tensor_scalar(out=out, in0=in0, scalar1=s1, scalar2=s2, op0=op0, op1=op1), reads, writes)

        def stt(out, in0, scalar, in1, op0, op1, reads, writes, accum=None):
            if accum is None:
                op("dve", lambda e: e.scalar_tensor_tensor(out=out, in0=in0, scalar=scalar, in1=in1, op0=op0, op1=op1), reads, writes)
            else:
                op("dve", lambda e: e.scalar_tensor_tensor(out=out, in0=in0, scalar=scalar, in1=in1, op0=op0, op1=op1, accum_out=accum), reads, writes)

        def cp(q_, out, in_, reads, writes):
            if q_ == "act":
                op(q_, lambda e: e.activation(out=out, in_=in_, func=AF.Identity), reads, writes)
            else:
                op(q_, lambda e: e.tensor_copy(out=out, in_=in_), reads, writes)

        def rsqrt_pool(out, in_, mult, reads, writes, post=None):
            ts_("pool", out, in_, mult, EPS, ALU.mult, ALU.add, reads, writes)
            tt_("pool", out, out, cmh[:, 0:1].to_broadcast(list(out.shape)), ALU.pow, list(writes) + ["cmh"], writes)
            if post is not None:
                ts_("pool", out, out, post, 1.0, ALU.mult, ALU.mult, writes, writes)

        def run_streams(*gens):
            gens = [g for g in gens]
            while gens:
                for g in list(gens):
                    try:
                        next(g)
                    except StopIteration:
                        gens.remove(g)

        def norm_to_h(gv_a, shv, tagks):
            for tb in range(4):
                act(junk[:], xt[:, tb, :], AF.Square, ["xt"], ["junk", f"ss{tb}"], accum=ss[:, tb:tb + 1])
            for tb in range(4):
                rsqrt_pool(rstd[:, tb:tb + 1], ss[:, tb:tb + 1], 1.0 / D, [f"ss{tb}"], [f"rstd{tb}"])
            for tb in range(4):
                xb = xs2[tb % 2]
                xk = f"xs{tb % 2}"
                ts_("dve", xb[:], xt[:, tb, :], rstd[:, tb:tb + 1], None, ALU.mult, None, ["xt", f"rstd{tb}"], [xk])
                b = nb4()
                pv = psb(b)
                for kc in range(8):
                    tr(pv[:, kc * 128:(kc + 1) * 128], xb[:, kc * 128:(kc + 1) * 128], ident[:], [xk, "ident"], [*pk(b)])
                for kc in range(8):
                    act(h[:, kc, tb * 128:(tb + 1) * 128], pv[:, kc * 128:(kc + 1) * 128], AF.Identity,
                        [*pk(b)] + tagks, ["h"], scale=gv_a[:, kc:kc + 1], bias=shv[:, kc:kc + 1])

        def gen_qkv(ti):
            s_lat = take("lat")
            lat = ring[s_lat]
            for tb in range(4):
                tsl = slice(tb * 128, (tb + 1) * 128)
                b = nbA()
                for kc in range(8):
                    mm(ps[b][:, 0:416], h[:, kc, tsl], lat[:, kc * 416:(kc + 1) * 416], kc == 0, kc == 7,
                       ["h", f"ring{s_lat}"], [*pk(b)])
                cp("dve", zlat[:], ps[b][:, 0:416], [*pk(b)], ["zlat"])
                yield
                act(junk2[:, 0:256], zlat[:, 0:256], AF.Square, ["zlat"], ["junk2", "st1"], accum=st1[:, 0:1])
                act(junk2[:, 256:384], zlat[:, 256:384], AF.Square, ["zlat"], ["junk2", "st1"], accum=st1[:, 1:2])
                act(junk2[:, 384:416], zlat[:, 384:416], AF.Square, ["zlat"], ["junk2", f"skr{tb}"], accum=sskr[:, tb:tb + 1])
                cp("pool", zkr[:, tb, :], zlat[:, 384:416], ["zlat"], [f"zkr{tb}"])
                rsqrt_pool(st2[:, 0:1], st1[:, 0:1], 1.0 / 256, ["st1"], ["st2"])
                rsqrt_pool(st2[:, 1:2], st1[:, 1:2], 1.0 / 128, ["st1"], ["st2"])
                yield
                stt(zn[:, 0:256], zlat[:, 0:256], st2[:, 0:1], gqlat_bc[:], ALU.mult, ALU.mult, ["zlat", "st2", "gqlat_bc"], ["zn"])
                stt(zn[:, 256:384], zlat[:, 256:384], st2[:, 1:2], gkvlat_bc[:], ALU.mult, ALU.mult, ["zlat", "st2", "gkvlat_bc"], ["zn"])
                b = nbAt()
                pv = psb(b)
                for c3 in range(3):
                    tr(pv[:, c3 * 128:(c3 + 1) * 128], zn[:, c3 * 128:(c3 + 1) * 128], ident[:], ["zn", "ident"], [*pk(b)])
                cp("dve", znT[:, :, tsl], pv[:, 0:384].rearrange("p (c t) -> p c t", c=3), [*pk(b)], [f"znT{tb}"])
                yield
            release(s_lat)
            s_uq = take("uqkv")
            uqp = ring[s_uq]
            for tb in range(4):
                blk = ti * 4 + tb
                tsl = slice(tb * 128, (tb + 1) * 128)
                bq = [nbA(), nbA()]
                for hf in range(2):
                    for kc in range(2):
                        mm(ps[bq[hf]][:, 0:384], znT[:, kc, tsl], uqp[:, kc * 768 + hf * 384: kc * 768 + hf * 384 + 384],
                           kc == 0, kc == 1, [f"znT{tb}", f"ring{s_uq}"], [*pk(bq[hf])])
                for hf in range(2):
                    act(sq[:, hf * 384:(hf + 1) * 384], ps[bq[hf]][:, 0:384], AF.Square, [*pk(bq[hf])], ["sq"])
                op("dve", lambda e: e.tensor_reduce(out=ssq[:], in_=sq[:].rearrange("p (h d) -> p h d", h=8), axis=AX.X, op=ALU.add),
                   ["sq"], ["ssq"])
                rsqrt_pool(r2[:], ssq[:], 1.0 / 96, ["ssq"], ["r2"])
                yield
                for hf in range(2):
                    tt_("dve", q[:, hf * 384:(hf + 1) * 384].rearrange("p (h d) -> p h d", h=4),
                        ps[bq[hf]][:, 0:384].rearrange("p (h d) -> p h d", h=4),
                        AP(r2, hf * 4, [[8, 128], [1, 4], [0, 96]]), ALU.mult, [*pk(bq[hf]), "r2"], ["q"])
                q3 = q[:].rearrange("p (h d) -> p h d", h=8)
                tt_("dve", q3, q3, AP(gq_bc, 0, [[96, 128], [0, 8], [1, 96]]), ALU.mult, ["q", "gq_bc"], ["q"])
                qb3 = qb[:].rearrange("p (h d) -> p h d", h=8)
                cosb = AP(cos_t, blk * 16, [[256, 128], [0, 8], [1, 16]])
                sinb = AP(sin_t, blk * 16, [[256, 128], [0, 8], [1, 16]])
                cp("pool", qb3[:, :, 0:64], q3[:, :, 0:64], ["q"], ["qb"])
                tt_("dve", rt[:], q3[:, :, 64:80], cosb, ALU.mult, ["q", "cos"], ["rt"])
                tt_("dve", rt2[:], q3[:, :, 80:96], sinb, ALU.mult, ["q", "sin"], ["rt2"])
                tt_("dve", qb3[:, :, 64:80], rt[:], rt2[:], ALU.subtract, ["rt", "rt2"], ["qb"])
                tt_("dve", rt[:], q3[:, :, 80:96], cosb, ALU.mult, ["q", "cos"], ["rt"])
                tt_("dve", rt2[:], q3[:, :, 64:80], sinb, ALU.mult, ["q", "sin"], ["rt2"])
                tt_("dve", qb3[:, :, 80:96], rt[:], rt2[:], ALU.add, ["rt", "rt2"], ["qb"])
                yield
                b = nbAt()
                pv = psb(b)
                for hh in range(8):
                    tr(pv[0:96, hh * 128:(hh + 1) * 128], qb[:, hh * 96:(hh + 1) * 96], ident[:], ["qb", "ident"], [*pk(b)])
                cp("dve", qm[0:96, :, tsl], pv[0:96, :].rearrange("p (h t) -> p h t", h=8), [*pk(b)], ["qm"])
                bk = [nbA(), nbA()]
                for hf in range(2):
                    mm(ps[bk[hf]][:, :], znT[:, 2, tsl], uqp[:, 1536 + hf * 512:1536 + (hf + 1) * 512], True, True,
                       [f"znT{tb}", f"ring{s_uq}"], [*pk(bk[hf])])
                for hf in range(2):
                    kv3 = ps[bk[hf]][:, :].rearrange("p (h d) -> p h d", h=4)
                    act(sq[:, hf * 256:(hf + 1) * 256].rearrange("p (h d) -> p h d", h=4), kv3[:, :, 0:64], AF.Square,
                        [*pk(bk[hf])], ["sq"])
                op("dve", lambda e: e.tensor_reduce(out=ssk[:], in_=sq[:, 0:512].rearrange("p (h d) -> p h d", h=8), axis=AX.X, op=ALU.add),
                   ["sq"], ["ssk"])
                ts_("pool", ssk[:], ssk[:], sskr[:, tb:tb + 1], 1.0, ALU.add, ALU.mult, ["ssk", f"skr{tb}"], ["ssk"])
                rsqrt_pool(kscale[:, blk, :], ssk[:], 1.0 / 96, ["ssk"], ["kscale"], post=96 ** -0.5)
                yield
                kb3 = kb[:].rearrange("p (h d) -> p h d", h=8)
                for hf in range(2):
                    kv3 = ps[bk[hf]][:, :].rearrange("p (h d) -> p h d", h=4)
                    tt_("dve", kb3[:, hf * 4:(hf + 1) * 4, 0:64], kv3[:, :, 0:64], AP(gk_bc, 0, [[96, 128], [0, 4], [1, 64]]), ALU.mult,
                        [*pk(bk[hf]), "gk_bc"], ["kb"])
                    cp("act", Vc[:, blk, hf * 256:(hf + 1) * 256].rearrange("p (h d) -> p h d", h=4), kv3[:, :, 64:128],
                       [*pk(bk[hf])], ["Vc"])
                tt_("dve", kr[:], zkr[:, tb, :], gk_bc[:, 64:96], ALU.mult, [f"zkr{tb}", "gk_bc"], ["kr"])
                c2 = cos_t[:, blk, :]
                s2 = sin_t[:, blk, :]
                tt_("dve", rt[:, 0, :], kr[:, 0:16], c2, ALU.mult, ["kr", "cos"], ["rt"])
                tt_("dve", rt2[:, 0, :], kr[:, 16:32], s2, ALU.mult, ["kr", "sin"], ["rt2"])
                tt_("dve", krb[:, 0:16], rt[:, 0, :], rt2[:, 0, :], ALU.subtract, ["rt", "rt2"], ["krb"])
                tt_("dve", rt[:, 0, :], kr[:, 16:32], c2, ALU.mult, ["kr", "cos"], ["rt"])
                tt_("dve", rt2[:, 0, :], kr[:, 0:16], s2, ALU.mult, ["kr", "sin"], ["rt2"])
                tt_("dve", krb[:, 16:32], rt[:, 0, :], rt2[:, 0, :], ALU.add, ["rt", "rt2"], ["krb"])
                cp("pool", kb3[:, :, 64:96], AP(krb, 0, [[32, 128], [0, 8], [1, 32]]), ["krb"], ["kb"])
                yield
                b = nbAt()
                pv = psb(b)
                for hh in range(8):
                    tr(pv[0:96, hh * 128:(hh + 1) * 128], kb[:, hh * 96:(hh + 1) * 96], ident[:], ["kb", "ident"], [*pk(b)])
                cp("dve", Kc[:, :, blk * 128:(blk + 1) * 128], pv[0:96, :].rearrange("p (h t) -> p h t", h=8), [*pk(b)], ["Kc"])
                yield
            release(s_uq)

        def gen_glu_gates_conv(ti):
            s_b = take("gluB")
            for c4 in range(4):
                bb = nbB()
                for kc in range(8):
                    mm(ps[bb][:, :], ring[s_b][:, kc * 512 + c4 * 128: kc * 512 + c4 * 128 + 128], h[:, kc, :], kc == 0, kc == 7,
                       [f"ring{s_b}", "h"], [*pk(bb)])
                act(tg[:, 12 + c4, :], ps[bb][:, :], AF.Tanh, [*pk(bb)], [f"tg{12 + c4}"], scale=0.5)
                yield
            release(s_b)
            s_a = take("gluA")
            for c4 in range(4):
                ba = nbB()
                for kc in range(8):
                    mm(ps[ba][:, :], ring[s_a][:, kc * 512 + c4 * 128: kc * 512 + c4 * 128 + 128], h[:, kc, :], kc == 0, kc == 7,
                       [f"ring{s_a}", "h"], [*pk(ba)])
                stt(uT[:, c4, 30:30 + T], tg[:, 12 + c4, :], 1.0, ps[ba][:, :], ALU.add, ALU.mult, [f"tg{12 + c4}", *pk(ba)], ["uT"])
                yield
            release(s_a)
            for g in range(4):
                s_g = take(f"gate{g}")
                for c4 in range(4):
                    b = nbB()
                    for kc in range(8):
                        mm(ps[b][:, :], ring[s_g][:, kc * 512 + c4 * 128: kc * 512 + c4 * 128 + 128], h[:, kc, :], kc == 0, kc == 7,
                           [f"ring{s_g}", "h"], [*pk(b)])
                    act(tg[:, g * 4 + c4, :], ps[b][:, :], AF.Tanh, [*pk(b)], [f"tg{g * 4 + c4}"], scale=0.5)
                    yield
                release(s_g)
            for cc in range(4):
                s_c = take(f"cdiag{cc}")
                for tb in range(4):
                    for j in range(31):
                        mm(ps[4 + tb][:, cc * 128:(cc + 1) * 128], uT[:, cc, tb * 128 + j: tb * 128 + j + 128],
                           ring[s_c][:, j * 128:(j + 1) * 128], j == 0, j == 30, ["uT", f"ring{s_c}"], [*pk(4 + tb)])
                    yield
                release(s_c)
            for tb in range(4):
                stt(cx4[:, tb, :], ps[4 + tb][:, :], 0.5, convb_bc[:], ALU.mult, ALU.add, [*pk(4 + tb), "convb_bc"], [f"cx{tb}", f"lnst{tb}"],
                    accum=lnst4[:, tb, 0:1])
            if ti == NT - 1:
                op("pool", lambda e: e.memset(uT[:, :, 0:30], 0.0), ["uT"], ["uT"])
            else:
                cp("pool", uT[:, :, 0:30], uT[:, :, T:T + 30], ["uT"], ["uT"])
            yield

        def gen_ln(ti):
            for tb in range(4):
                tsl = slice(tb * 128, (tb + 1) * 128)
                ln = lnst4[:, tb, :]
                lk = f"lnst{tb}"
                act(junk2[:, 0:512], cx4[:, tb, :], AF.Square, [f"cx{tb}"], ["junk2", lk], accum=ln[:, 1:2])
                ts_("pool", ln[:, 0:1], ln[:, 0:1], 1.0 / 512, 1.0, ALU.mult, ALU.mult, [lk], [lk])
                tt_("pool", ln[:, 2:3], ln[:, 0:1], ln[:, 0:1], ALU.mult, [lk], [lk])
                ts_("pool", ln[:, 1:2], ln[:, 1:2], 1.0 / 512, ln[:, 2:3], ALU.mult, ALU.subtract, [lk], [lk])
                rsqrt_pool(ln[:, 1:2], ln[:, 1:2], 1.0, [lk], [lk])
                tt_("pool", ln[:, 3:4], ln[:, 0:1], ln[:, 1:2], ALU.mult, [lk], [lk])
                ts_("pool", ln[:, 3:4], ln[:, 3:4], -1.0, 1.0, ALU.mult, ALU.mult, [lk], [lk])
                yield
                act(xn[:], cx4[:, tb, :], AF.Identity, [f"cx{tb}", lk], ["xn"], scale=ln[:, 1:2], bias=ln[:, 3:4])
                b = nbL()
                pv = psb(b)
                for c4 in range(4):
                    tr(pv[:, c4 * 128:(c4 + 1) * 128], xn[:, c4 * 128:(c4 + 1) * 128], ident[:], ["xn", "ident"], [*pk(b)])
                yield
                for c4 in range(4):
                    act(yy[:, c4 * 128:(c4 + 1) * 128], pv[:, c4 * 128:(c4 + 1) * 128], AF.Identity, [*pk(b), "lng", "lnb"], ["yy"],
                        scale=lng[:, c4:c4 + 1], bias=lnb[:, c4:c4 + 1])
                act(t2[:], yy[:], AF.Tanh, ["yy"], ["t2"], scale=0.5)
                stt(u2T[:, :, tsl], t2[:].rearrange("p (c t) -> p c t", c=4), 1.0, yy[:].rearrange("p (c t) -> p c t", c=4),
                    ALU.add, ALU.mult, ["t2", "yy"], ["u2T"])
                yield

        def gen_attn(ti):
            nkb = 4 * ti + 4
            for hh in range(NH):
                po = 64 * (hh % 2)
                bacc = 4 + ((hh // 2) % 2) * 2
                bden = bacc + 1
                info = {}

                def s_stage(kbi):
                    j = kbi - 4 * ti
                    c0 = 128 * j if j >= 0 else 0
                    b = nbS()
                    mm(ps[b][:, c0:T], Kc[:, hh, kbi * 128:(kbi + 1) * 128], qm[0:96, hh, c0:T], True, True,
                       ["Kc", "qm"], [*pk(b)])
                    ksc = kscale[:, kbi, hh:hh + 1]
                    if j < 0:
                        pt = PT[kbi % 4]
                        ptk = f"PT{kbi % 4}"
                        act(pt[:, :], ps[b][:, :], AF.Exp, [*pk(b), "kscale"], [ptk], scale=ksc)
                    else:
                        pt = PD[j]
                        ptk = f"PD{j}"
                        if c0 + 64 < T:
                            act(pt[:, c0 + 64:T], ps[b][:, c0 + 64:T], AF.Exp, [*pk(b), "kscale"], [ptk], scale=ksc)
                        act(pt[0:64, c0:c0 + 64], ps[b][0:64, c0:c0 + 64], AF.Exp, [*pk(b), "kscale"], [ptk], scale=ksc[0:64, :])
                    info[kbi] = (pt, ptk, c0)

                def pv_stage(kbi):
                    pt, ptk, c0 = info[kbi]
                    mm(ps[bacc][po:po + 64, c0:T], Vc[:, kbi, hh * 64:(hh + 1) * 64], pt[:, c0:T], kbi == 0, kbi == nkb - 1,
                       [ptk, "Vc"], [hk(bacc, po)])
                    mm(ps[bden][po:po + 64, c0:T], ones64[:, :], pt[:, c0:T], kbi == 0, kbi == nkb - 1,
                       [ptk, "ones64"], [hk(bden, po)])

                for idx in range(nkb + 1):
                    if idx < nkb:
                        s_stage(idx)
                    if idx >= 1:
                        pv_stage(idx - 1)
                    yield
                op("dve", lambda e, bden=bden, po=po: e.reciprocal(out=recip[po:po + 64, :], in_=ps[bden][po:po + 64, :]),
                   [hk(bden, po)], [f"recip{po}"])
                tt_("dve", attnT[po:po + 64, hh // 2, :], ps[bacc][po:po + 64, :], recip[po:po + 64, :], ALU.mult,
                    [hk(bacc, po), f"recip{po}"], ["attnT"])
                yield

        def emit_tile(bi, ti):
            t0 = ti * T
            PR[0].phase = f"ld:{bi}:{ti}"
            PR[0].dma("sp", "xl", lambda e, bi=bi, t0=t0: e.dma_start(
                out=xt[:], in_=x_d[bi, t0:t0 + T, :].rearrange("(tb p) d -> p tb d", p=128)),
                reads=[], writes=["xt"])
            PR[0].phase = f"norm1:{bi}:{ti}"
            norm_to_h(a1, sh1, ["a1", "sh1"])
            PR[0].phase = f"qkvB:{bi}:{ti}"
            run_streams(gen_qkv(ti), gen_glu_gates_conv(ti))
            PR[0].phase = f"attn:{bi}:{ti}"
            run_streams(gen_attn(ti), gen_ln(ti))
            PR[0].phase = f"merge:{bi}:{ti}"
            s_pw = take("pw")
            s_o = take("wo")
            for mc in range(8):
                ba, bb = nb8(), nb8()
                for kc in range(4):
                    mm(ps[ba][:, :], ring[s_o][:, kc * 1024 + mc * 128: kc * 1024 + mc * 128 + 128], attnT[:, kc, :],
                       kc == 0, kc == 3, [f"ring{s_o}", "attnT"], [*pk(ba)])
                for kc in range(4):
                    mm(ps[bb][:, :], ring[s_pw][:, kc * 1024 + mc * 128: kc * 1024 + mc * 128 + 128], u2T[:, kc, :],
                       kc == 0, kc == 3, [f"ring{s_pw}", "u2T"], [*pk(bb)])
                stt(tmpA[:], tg[:, mc, :], 1.0, ps[ba][:, :], ALU.add, ALU.mult, [f"tg{mc}", *pk(ba)], ["tmpA"])
                stt(tmpB[:], tg[:, 8 + mc, :], 1.0, ps[bb][:, :], ALU.add, ALU.mult, [f"tg{8 + mc}", *pk(bb)], ["tmpB"])
                stt(qm[:, mc, :], tmpB[:], 0.5, tmpA[:], ALU.mult, ALU.add, ["tmpA", "tmpB"], ["qm"])
            release(s_pw)
            release(s_o)
            PR[0].phase = f"wout:{bi}:{ti}"
            for hf in range(2):
                s_w = take(f"wout{hf}")
                for tb in range(4):
                    b = nb8()
                    for kc in range(8):
                        mm(ps[b][:, :], qm[:, kc, tb * 128:(tb + 1) * 128], ring[s_w][:, kc * 512:(kc + 1) * 512], kc == 0, kc == 7,
                           ["qm", f"ring{s_w}"], [*pk(b)])
                    tt_("dve", tmpA[:], ps[b][:, :], gate1h[:, hf * 512:(hf + 1) * 512], ALU.mult, [*pk(b), "gate1h"], ["tmpA"])
                    tt_("dve", xt[:, tb, hf * 512:(hf + 1) * 512], xt[:, tb, hf * 512:(hf + 1) * 512], tmpA[:], ALU.add,
                        ["xt", "tmpA"], ["xt"])
                release(s_w)
            PR[0].phase = f"norm2:{bi}:{ti}"
            norm_to_h(a2, sh2, ["a2", "sh2"])
            PR[0].phase = f"ffn:{bi}:{ti}"
            for dh in range(2):
                for p in range(4):
                    s_f = take(f"ff1_{dh}_{p}")
                    for c4 in range(4):
                        b = nb8()
                        for kc in range(8):
                            mm(ps[b][:, :], ring[s_f][:, kc * 512 + c4 * 128: kc * 512 + c4 * 128 + 128], h[:, kc, :], kc == 0, kc == 7,
                               [f"ring{s_f}", "h"], [*pk(b)])
                        tb_ = tmpB if c4 % 2 == 0 else tmpA
                        tk = "tmpB" if c4 % 2 == 0 else "tmpA"
                        act(tb_[:], ps[b][:, :], AF.Relu, [*pk(b)], [tk])
                        tt_("dve", tg[:, p * 4 + c4, :], tb_[:], tb_[:], ALU.mult, [tk], [f"tg{p * 4 + c4}"])
                    release(s_f)
                for p in range(4):
                    s_f = take(f"ff2_{dh}_{p}")
                    for k4 in range(4):
                        kc = p * 4 + k4
                        for tb in range(4):
                            for hf in range(2):
                                bo = tb * 2 + hf
                                mm(ps[bo][:, :], tg[:, kc, tb * 128:(tb + 1) * 128],
                                   ring[s_f][:, k4 * 1024 + hf * 512: k4 * 1024 + hf * 512 + 512], kc == 0, kc == 15,
                                   [f"tg{kc}", f"ring{s_f}"], [*pk(bo)])
                    release(s_f)
                for tb in range(4):
                    for hf in range(2):
                        bo = tb * 2 + hf
                        tb_ = tmpB if hf % 2 == 0 else tmpA
                        tk = "tmpB" if hf % 2 == 0 else "tmpA"
                        tt_("dve", tb_[:], ps[bo][:, :], gate2b[:, hf * 512:(hf + 1) * 512], ALU.mult, [*pk(bo), "gate2b"], [tk])
                        tt_("pool" if dh == 0 else "dve", xt[:, tb, hf * 512:(hf + 1) * 512], xt[:, tb, hf * 512:(hf + 1) * 512], tb_[:], ALU.add,
                            ["xt", tk], ["xt"])
            PR[0].phase = f"st:{bi}:{ti}"
            oi = PR[0].dma("sp", "xo", lambda e, bi=bi, t0=t0: e.dma_start(
                out=out_d[bi, t0:t0 + T, :].rearrange("(tb p) d -> p tb d", p=128), in_=xt[:]),
                reads=["xt"], writes=["out_d"])
            out_ops.append(oi)

        PR[0] = Prog(nc)
        emit_tile(0, 1)
        stt_["seq"] = list(stt_["rec"])
        assert sorted(stt_["seq"]) == sorted(pname.keys()), (stt_["seq"], list(pname.keys()))
        PR[0] = P
        out_ops.clear()

        for bi in range(BPC):
            def mrow(k):
                return mod_d[bi, k * D:(k + 1) * D]
            P.phase = f"seq:{bi}"
            sload(sh1[:], mrow(0).rearrange("(kc p) -> p kc", p=128), ["sh1"], reads=["mod_d"])
            sload(sc1[:], mrow(1).rearrange("(kc p) -> p kc", p=128), ["sc1"], reads=["mod_d"])
            sload(sh2[:], mrow(3).rearrange("(kc p) -> p kc", p=128), ["sh2"], reads=["mod_d"])
            sload(sc2[:], mrow(4).rearrange("(kc p) -> p kc", p=128), ["sc2"], reads=["mod_d"])
            P.dma("sp", "g1", lambda e, bi=bi: e.dma_start(out=gate1h[:], in_=AP(mod_d.tensor, bi * 6 * D + 2 * D, [[0, 128], [1, D]])),
                  reads=["mod_d"], writes=["gate1h"])
            P.dma("sp", "g2", lambda e, bi=bi: e.dma_start(out=gate2b[:], in_=AP(mod_d.tensor, bi * 6 * D + 5 * D, [[0, 128], [1, D]])),
                  reads=["mod_d"], writes=["gate2b"])
            ts_("dve", gate1h[:], gate1h[:], 0.5, None, ALU.mult, None, ["gate1h"], ["gate1h"])
            stt(a1[:], sc1[:], 1.0, g1v[:], ALU.add, ALU.mult, ["sc1", "g1v"], ["a1"])
            stt(a2[:], sc2[:], 1.0, g2v[:], ALU.add, ALU.mult, ["sc2", "g2v"], ["a2"])
            for ti in range(NT):
                emit_tile(bi, ti)

        P.emit(final_wait_ops=out_ops[-1:])
    return nc, P


def _rope_tables():
    inv_freq = (np.float32(10000.0) ** (-np.arange(0, 32, 2, dtype=np.float32) / np.float32(32))).astype(np.float32)
    ang = (np.arange(S, dtype=np.float32)[:, None] * inv_freq[None, :]).astype(np.float32)
    return np.cos(ang).astype(np.float32), np.sin(ang).astype(np.float32)


_CACHE = {}


def kernel(**inputs):
    if "nc" not in _CACHE:
        _CACHE["nc"] = build_nc()
    nc, _ = _CACHE["nc"]
    cos, sin = _rope_tables()
    ident = np.eye(128, dtype=np.float32)
    x = np.ascontiguousarray(np.asarray(inputs["x"], dtype=np.float32))
    c = np.ascontiguousarray(np.asarray(inputs["c"], dtype=np.float32))
    shared = {}
    for k, v in inputs.items():
        if k in ("x", "c"):
            continue
        a = np.asarray(v, dtype=np.float32)
        shared[k] = np.ascontiguousarray(a[0] if a.ndim == 3 else a)
    shared["rope_cos"] = cos
    shared["rope_sin"] = sin
    shared["ident"] = ident
    in_maps = []
    for i in range(NCORES):
        m = dict(shared)
        m["x"] = x[i * BPC:(i + 1) * BPC]
        m["c"] = c[i * BPC:(i + 1) * BPC]
        in_maps.append(m)
    res = run_bass_kernel_spmd(nc, in_maps, core_ids=list(range(NCORES)))
    return np.concatenate([r["out"] for r in res.results], axis=0)
```

```python
import contextlib
import os
import numpy as np
import concourse.bass as bass
import concourse.mybir as mybir
from concourse.bass_utils import run_bass_kernel_spmd

F32 = mybir.dt.float32
BF16 = mybir.dt.bfloat16
AF = mybir.ActivationFunctionType
ALU = mybir.AluOpType
AX = mybir.AxisListType

COMPUTE = ("pe", "act", "dve", "pool")
QUEUES = ("pe", "act", "dve", "pool", "sp")

NCORES = 8
D = 1024
S = 2048
BPC = 4
T = 512
NT = S // T
NH = 8
EPS = 1e-6
PANEL = 4096
F_DEPTH = int(os.environ.get("F_DEPTH", "2"))
F_NORMDVE = int(os.environ.get("F_NORMDVE", "1"))
F_ACTCP = int(os.environ.get("F_ACTCP", "1"))
F_XTB = int(os.environ.get("F_XTB", "1"))
F_WOUT = int(os.environ.get("F_WOUT", "1"))
RING = 3


class Prog:
    def __init__(self, nc, epoch_len=2000):
        self.nc = nc
        self.ops = []
        self.per_q = {q: [] for q in QUEUES}
        self.last_w = {}
        self.readers = {}
        self.stream_last = {}
        self.stream_cnt = {}
        self.epoch_len = epoch_len
        self.phase = ""
        self.annotate = False

    def _deps(self, reads, writes, me):
        deps = set()
        for r in reads:
            w = self.last_w.get(r)
            if w is not None:
                deps.add(w)
        for w_ in writes:
            w = self.last_w.get(w_)
            if w is not None:
                deps.add(w)
            for rd in self.readers.get(w_, ()):
                deps.add(rd)
        deps.discard(me)
        return deps

    def _commit(self, reads, writes, me):
        mo = self.ops[me]
        for r in reads:
            lst = self.readers.setdefault(r, [])
            if mo["kind"] == "c":
                lst[:] = [x for x in lst if not (self.ops[x]["kind"] == "c" and self.ops[x]["q"] == mo["q"])]
            lst.append(me)
        for w_ in writes:
            self.last_w[w_] = me
            self.readers[w_] = []

    def op(self, q, fn, reads=(), writes=()):
        pr = [r for r in reads if r.startswith("ps")]
        if pr:
            writes = list(writes) + [r for r in pr if r not in writes]
        idx = len(self.ops)
        deps = self._deps(reads, writes, idx)
        if q == "pe":
            deps = {d for d in deps if not (self.ops[d]["kind"] == "c" and self.ops[d]["q"] == "pe")}
        self.ops.append(dict(kind="c", q=q, fn=fn, deps=deps, sig=False, ph=self.phase))
        self._commit(reads, writes, idx)
        return idx

    def dma(self, q, stream, fn, reads=(), writes=()):
        idx = len(self.ops)
        deps = self._deps(reads, writes, idx)
        prev = self.stream_last.get(stream)
        if prev is not None:
            deps.add(prev)
        n = self.stream_cnt.get(stream, 0) + 1
        self.stream_cnt[stream] = n
        self.stream_last[stream] = idx
        self.ops.append(dict(kind="d", q=q, fn=fn, deps=deps, stream=stream, val=16 * n, sig=True, ph=self.phase))
        self._commit(reads, writes, idx)
        return idx

    def emit(self, final_wait_ops=()):
        nc = self.nc
        ops = self.ops
        for o in ops:
            for d in o["deps"]:
                if ops[d]["kind"] == "c":
                    ops[d]["sig"] = True
        signum = {}
        cnt = {q: 0 for q in COMPUTE}
        for i, o in enumerate(ops):
            if o["kind"] == "c" and o["sig"]:
                cnt[o["q"]] += 1
                signum[i] = cnt[o["q"]]
        stack = contextlib.ExitStack()
        EL = self.epoch_len
        sems = {}
        for q in COMPUTE:
            ne = cnt[q] // EL + 1
            sems[q] = [stack.enter_context(nc.semaphore(f"s_{q}_{e}")) for e in range(ne)]
        ssem = {s: stack.enter_context(nc.semaphore(f"d_{s}")) for s in self.stream_cnt}
        clock = {q: {} for q in QUEUES}
        snap = {}
        plans = {q: [] for q in QUEUES}
        for i, o in enumerate(ops):
            q = o["q"]
            ck = clock[q]
            wm = {}
            for d in sorted(o["deps"]):
                od = ops[d]
                if od["kind"] == "c":
                    key, v = od["q"], signum[d]
                else:
                    key, v = "S:" + od["stream"], od["val"]
                if ck.get(key, 0) >= v:
                    continue
                wm[key] = max(wm.get(key, 0), v)
                for k2, v2 in snap[d].items():
                    if ck.get(k2, 0) < v2:
                        ck[k2] = v2
                if ck.get(key, 0) < v:
                    ck[key] = v
            s = dict(ck)
            if o["kind"] == "c":
                if o["sig"]:
                    s[q] = max(s.get(q, 0), signum[i])
            else:
                s["S:" + o["stream"]] = o["val"]
            snap[i] = s
            plans[q].append((i, wm))
        self.stats = dict(cnt=cnt, nops=len(ops), nsem=sum(len(v) for v in sems.values()) + len(ssem))

        def run(q, eng):
            for i, wm in plans[q]:
                o = ops[i]
                for k, v in wm.items():
                    if k.startswith("S:"):
                        eng.wait_ge(ssem[k[2:]], v)
                    else:
                        e = (v - 1) // EL
                        eng.wait_ge(sems[k][e], v - e * EL)
                ins = o["fn"](eng)
                if self.annotate:
                    ins.annotate(o["ph"])
                if o["kind"] == "d":
                    ins.then_inc(ssem[o["stream"]], 16)
                elif o["sig"]:
                    e = (signum[i] - 1) // EL
                    ins.then_inc(sems[q][e], 1)
            if q == "sp":
                for d in final_wait_ops:
                    od = ops[d]
                    eng.wait_ge(ssem[od["stream"]], od["val"])

        with stack:
            with nc.Block() as block:
                @block.tensor
                def _(e):
                    run("pe", e)

                @block.scalar
                def _(e):
                    run("act", e)

                @block.vector
                def _(e):
                    run("dve", e)

                @block.gpsimd
                def _(e):
                    run("pool", e)

                @block.sync
                def _(e):
                    run("sp", e)


def panel_list():
    pl = []
    pl.append(("lat", "cols", ("w_in", 8, 0, 416)))
    pl.append(("uqkv", "uqkv", None))
    pl.append(("gluB", "cols", ("w_in", 8, 928, 512)))
    pl.append(("gluA", "cols", ("w_in", 8, 416, 512)))
    for g in range(4):
        pl.append((f"gate{g}", "cols", ("w_in", 8, 1440 + 512 * g, 512)))
    for cc in range(4):
        pl.append((f"cdiag{cc}", "cdiag", cc))
    pl.append(("pw", "cols", ("w_pw_out", 4, 0, 1024)))
    pl.append(("wo", "cols", ("w_o_mla", 4, 0, 1024)))
    pl.append(("wout0", "cols", ("w_out", 8, 0, 512)))
    pl.append(("wout1", "cols", ("w_out", 8, 512, 512)))
    for dh in range(2):
        for p in range(4):
            pl.append((f"ff1_{dh}_{p}", "cols", ("w_ff1", 8, 2048 * dh + 512 * p, 512)))
        for p in range(4):
            pl.append((f"ff2_{dh}_{p}", "rows", ("w_ff2", 16 * dh + 4 * p, 4, 1024)))
    return pl


def build_nc(BPC=BPC, NT=NT, annotate=False):
    nc = bass.Bass("TRN2", target_bir_lowering=False)
    P = Prog(nc)
    P.annotate = annotate
    P.phase = "pro"

    def din(name, shape, dt=F32):
        return nc.dram_tensor(name, list(shape), dt, kind="ExternalInput").ap()

    x_d = din("x", [BPC, S, D])
    c_d = din("c", [BPC, D])
    w_ada = din("w_ada", [D, 6 * D])
    b_ada = din("b_ada", [1, 6 * D])
    norm1_g = din("norm1_g", [1, D])
    w_in = din("w_in", [D, 3488])
    q_latent_g = din("q_latent_g", [1, 256])
    w_uq = din("w_uq", [256, 768])
    kv_latent_g = din("kv_latent_g", [1, 128])
    w_ukv = din("w_ukv", [128, 1024])
    qk_q_g = din("qk_norm_q_g", [1, 96])
    qk_k_g = din("qk_norm_k_g", [1, 96])
    w_o_mla = din("w_o_mla", [512, D])
    conv_w = din("conv_w", [31, 512])
    conv_b = din("conv_b", [1, 512])
    conv_ln_g = din("conv_ln_g", [1, 512])
    conv_ln_b = din("conv_ln_b", [1, 512])
    w_pw_out = din("w_pw_out", [512, D])
    w_out = din("w_out", [D, D])
    norm2_g = din("norm2_g", [1, D])
    w_ff1 = din("w_ff1", [D, 4 * D])
    w_ff2 = din("w_ff2", [4 * D, D])
    cos_d = din("rope_cos", [S, 16])
    sin_d = din("rope_sin", [S, 16])
    ident_d = din("ident", [128, 128])
    wsrc = dict(w_in=w_in, w_pw_out=w_pw_out, w_out=w_out, w_ff1=w_ff1, w_ff2=w_ff2, w_o_mla=w_o_mla)

    out_d = nc.dram_tensor("out", [BPC, S, D], F32, kind="ExternalOutput").ap()
    panels = panel_list()
    NP = len(panels)
    wb_d = nc.dram_tensor("wb_scratch", [NP, 128, PANEL], BF16).ap()
    mod_d = nc.dram_tensor("mod_scratch", [BPC, 6 * D], F32).ap()

    st = contextlib.ExitStack()

    def sb(name, shape, dt):
        return st.enter_context(nc.sbuf_tensor(name, list(shape), dt))

    def pst(name):
        return st.enter_context(nc.psum_tensor(name, [128, 512], F32))

    def AP(t, off, dims):
        return bass.AP(t, off, [list(d) for d in dims])

    with st:
        ring = [sb(f"ring{i}", [128, PANEL], BF16) for i in range(RING)]
        xt = sb("xt", [128, 4, D], F32)
        xs2 = [sb(f"xs{i}", [128, D], BF16) for i in range(2)]
        junk2 = sb("junk2", [128, 512], BF16)
        junk = sb("junk", [128, D], BF16)
        h = sb("h", [128, 8, T], BF16)
        zlat = sb("zlat", [128, 416], F32)
        zn = sb("zn", [128, 384], BF16)
        znT = sb("znT", [128, 3, T], BF16)
        q = sb("q", [128, 768], F32)
        sq = sb("sq", [128, 768], F32)
        qb = sb("qb", [128, 768], BF16)
        qm = sb("qm", [128, 8, T], BF16)
        kb = sb("kb", [128, 768], BF16)
        Kc = sb("Kc", [96, NH, S], BF16)
        Vc = sb("Vc", [128, 16, 512], BF16)
        kscale = sb("kscale", [128, 16, NH], F32)
        PT = [sb(f"PT{i}", [128, T], BF16) for i in range(4)]
        PD = [sb(f"PD{i}", [128, T], BF16) for i in range(4)]
        attnT = sb("attnT", [128, 4, T], BF16)
        recip = sb("recip", [128, T], F32)
        tt = sb("tt", [128, T], F32)
        uT = sb("uT", [128, 4, 30 + T], BF16)
        cx4 = sb("cx4", [128, 4, T], F32)
        lnst4 = sb("lnst4", [128, 4, 4], F32)
        zkr = sb("zkr", [128, 4, 32], F32)
        sskr = sb("sskr", [128, 4], F32)
        xn = sb("xn", [128, T], BF16)
        yy = sb("yy", [128, T], F32)
        t2 = sb("t2", [128, T], F32)
        u2T = sb("u2T", [128, 4, T], BF16)
        tg = sb("tg", [128, 16, T], BF16)
        tmpA = sb("tmpA", [128, T], F32)
        tmpB = sb("tmpB", [128, T], F32)
        gate1h = sb("gate1h", [128, D], F32)
        gate2b = sb("gate2b", [128, D], F32)
        convb_bc = sb("convb_bc", [128, 512], F32)
        gqlat_bc = sb("gqlat_bc", [128, 256], F32)
        gkvlat_bc = sb("gkvlat_bc", [128, 128], F32)
        gq_bc = sb("gq_bc", [128, 96], F32)
        gk_bc = sb("gk_bc", [128, 96], F32)
        cos_t = sb("cos_t", [128, 16, 16], F32)
        sin_t = sb("sin_t", [128, 16, 16], F32)
        ident_f = sb("ident_f", [128, 128], F32)
        ident = sb("ident_b", [128, 128], BF16)
        ones64 = sb("ones64_b", [128, 64], BF16)
        g1v = sb("g1v", [128, 8], F32)
        g2v = sb("g2v", [128, 8], F32)
        sc1 = sb("sc1", [128, 8], F32)
        sh1 = sb("sh1", [128, 8], F32)
        sc2 = sb("sc2", [128, 8], F32)
        sh2 = sb("sh2", [128, 8], F32)
        a1 = sb("a1", [128, 8], F32)
        a2 = sb("a2", [128, 8], F32)
        lng = sb("lng", [128, 4], F32)
        lnb = sb("lnb", [128, 4], F32)
        cwT = sb("cwT", [128, 4, 31], F32)
        ss = sb("ss", [128, 8], F32)
        rstd = sb("rstd", [128, 8], F32)
        st1 = sb("st1", [128, 4], F32)
        st2 = sb("st2", [128, 4], F32)
        ssq = sb("ssq", [128, 8], F32)
        r2 = sb("r2", [128, 8], F32)
        ssk = sb("ssk", [128, 8], F32)
        cmh = sb("cmh", [128, 1], F32)
        kr = sb("kr", [128, 32], F32)
        krb = sb("krb", [128, 32], BF16)
        rt = sb("rt", [128, 8, 16], F32)
        rt2 = sb("rt2", [128, 8, 16], F32)
        cT = sb("cT", [128, 8, BPC], F32)
        cT2 = sb("cT2", [128, 8, BPC], F32)
        modrow = sb("modrow", [BPC, 512], F32)
        bada = sb("bada", [BPC, 512], F32)
        ps = [pst(f"ps{i}") for i in range(8)]

        def psb(i):
            return ps[i].bitcast(BF16)

        def mm(out, lhsT, rhs, start, stop, reads, writes):
            P.op("pe", lambda e: e.matmul(out, lhsT=lhsT, rhs=rhs, start=start, stop=stop), reads=reads, writes=writes)

        def tr(out, in_, idn, reads, writes):
            P.op("pe", lambda e: e.transpose(out, in_, idn), reads=reads, writes=writes)

        def act(out, in_, func, reads, writes, scale=1.0, bias=0.0, accum=None):
            if accum is None:
                P.op("act", lambda e: e.activation(out=out, in_=in_, func=func, bias=bias, scale=scale), reads=reads, writes=writes)
            else:
                P.op("act", lambda e: e.activation(out=out, in_=in_, func=func, bias=bias, scale=scale, accum_out=accum), reads=reads, writes=writes)

        def tt_(q_, out, in0, in1, op, reads, writes):
            P.op(q_, lambda e: e.tensor_tensor(out=out, in0=in0, in1=in1, op=op), reads=reads, writes=writes)

        def ts_(q_, out, in0, s1, s2, op0, op1, reads, writes):
            if op1 is None:
                P.op(q_, lambda e: e.tensor_scalar(out=out, in0=in0, scalar1=s1, scalar2=None, op0=op0), reads=reads, writes=writes)
            else:
                P.op(q_, lambda e: e.tensor_scalar(out=out, in0=in0, scalar1=s1, scalar2=s2, op0=op0, op1=op1), reads=reads, writes=writes)

        def stt(out, in0, scalar, in1, op0, op1, reads, writes, accum=None):
            if accum is None:
                P.op("dve", lambda e: e.scalar_tensor_tensor(out=out, in0=in0, scalar=scalar, in1=in1, op0=op0, op1=op1), reads=reads, writes=writes)
            else:
                P.op("dve", lambda e: e.scalar_tensor_tensor(out=out, in0=in0, scalar=scalar, in1=in1, op0=op0, op1=op1, accum_out=accum), reads=reads, writes=writes)

        def cp(q_, out, in_, reads, writes):
            if q_ == "act":
                P.op(q_, lambda e: e.activation(out=out, in_=in_, func=AF.Identity), reads=reads, writes=writes)
            else:
                P.op(q_, lambda e: e.tensor_copy(out=out, in_=in_), reads=reads, writes=writes)

        def rsqrt_pool(out, in_, mult, reads, writes, post=None):
            ts_("pool", out, in_, mult, EPS, ALU.mult, ALU.add, reads, writes)
            n = out.shape[-1] if len(out.shape) > 1 else 1
            tt_("pool", out, out, cmh[:, 0:1].to_broadcast(list(out.shape)), ALU.pow, list(writes) + ["cmh"], writes)
            if post is not None:
                ts_("pool", out, out, post, 1.0, ALU.mult, ALU.mult, writes, writes)

        sload_i = [0]

        def sload(out, in_, writes, q_="sp", reads=()):
            sload_i[0] += 1
            P.dma(q_, f"sl{sload_i[0] % 2}", lambda e: e.dma_start(out=out, in_=in_, allow_slow_non_contiguous=True), reads=list(reads), writes=writes)

        P.op("pool", lambda e: e.memset(cmh[:], -0.5), writes=["cmh"])
        P.op("pool", lambda e: e.memset(ones64[:], 1.0), writes=["ones64"])
        for i in range(4):
            P.op("pool", lambda e, i=i: e.memset(PD[i][:], 0.0), writes=[f"PD{i}"])
        P.op("pool", lambda e: e.memset(uT[:], 0.0), writes=["uT"])
        for i in range(RING):
            P.op("pool", lambda e, i=i: e.memset(ring[i][:, :], 0.0), writes=[f"ring{i}"])
        sload(ident_f[:], ident_d, ["ident_f"])
        cp("dve", ident[:], ident_f[:], ["ident_f"], ["ident"])
        sload(cos_t[:], cos_d.rearrange("(b p) j -> p b j", p=128), ["cos"])
        sload(sin_t[:], sin_d.rearrange("(b p) j -> p b j", p=128), ["sin"])
        sload(g1v[:], norm1_g.rearrange("o (kc p) -> p (o kc)", p=128), ["g1v"])
        sload(g2v[:], norm2_g.rearrange("o (kc p) -> p (o kc)", p=128), ["g2v"])
        sload(lng[:], conv_ln_g.rearrange("o (kc p) -> p (o kc)", p=128), ["lng"])
        sload(lnb[:], conv_ln_b.rearrange("o (kc p) -> p (o kc)", p=128), ["lnb"])
        for cc_ in range(4):
            sload(cwT[:, cc_, :], conv_w[:, cc_ * 128:(cc_ + 1) * 128].rearrange("j p -> p j"), ["cwT"])
        sload(convb_bc[:], conv_b.partition_broadcast(128) if False else AP(conv_b.tensor, 0, [[0, 128], [1, 512]]), ["convb_bc"])
        sload(gqlat_bc[:], AP(q_latent_g.tensor, 0, [[0, 128], [1, 256]]), ["gqlat_bc"])
        sload(gkvlat_bc[:], AP(kv_latent_g.tensor, 0, [[0, 128], [1, 128]]), ["gkvlat_bc"])
        sload(gq_bc[:], AP(qk_q_g.tensor, 0, [[0, 128], [1, 96]]), ["gq_bc"])
        sload(gk_bc[:], AP(qk_k_g.tensor, 0, [[0, 128], [1, 96]]), ["gk_bc"])
        for b_ in range(BPC):
            sload(cT[:, :, b_], c_d[b_, :].rearrange("(kc p) -> p kc", p=128), ["cT"])

        act(cT2[:], cT[:], AF.Tanh, ["cT"], ["cT2"], scale=0.5)
        stt(cT2[:], cT2[:], 1.0, cT[:], ALU.add, ALU.mult, ["cT2", "cT"], ["cT2"])
        ts_("dve", cT2[:], cT2[:], 0.5, None, ALU.mult, None, ["cT2"], ["cT2"])
        TGK = [f"tg{i}" for i in range(16)]
        stage = tg.reshape([128, 8192]).bitcast(F32)
        for g in range(12):
            P.dma("sp", "xl3", lambda e, g=g: e.dma_start(
                out=stage[:, :].rearrange("p (kc c) -> p kc c", kc=8),
                in_=w_ada[:, g * 512:(g + 1) * 512].rearrange("(kc p) c -> p kc c", p=128)),
                reads=[], writes=TGK)
            for kc in range(8):
                mm(ps[0][0:BPC, :], cT2[:, kc, :], stage[:, kc * 512:(kc + 1) * 512], kc == 0, kc == 7,
                   ["cT2"] + TGK, ["ps0"])
            sload(bada[:], AP(b_ada.tensor, g * 512, [[0, BPC], [1, 512]]), ["bada"])
            tt_("dve", modrow[:], ps[0][0:BPC, :], bada[:], ALU.add, ["ps0", "bada"], ["modrow"])
            P.dma("sp", "xo3", lambda e, g=g: e.dma_start(out=mod_d[:, g * 512:(g + 1) * 512], in_=modrow[:]),
                  reads=["modrow"], writes=["mod_d"])

        def fill_panel(pi, slot):
            name, kind, a = panels[pi]
            rk = f"ring{slot}"
            dst = ring[slot]
            if kind == "cols":
                src, nkc, c0, cw = a
                P.dma("pool", f"cv{slot}", lambda e: e.dma_start(
                    out=dst[:, 0:nkc * cw].rearrange("p (kc c) -> p kc c", kc=nkc),
                    in_=wsrc[src][:, c0:c0 + cw].rearrange("(kc p) c -> p kc c", p=128)),
                    reads=[], writes=[rk])
            elif kind == "rows":
                src, k0, nk, cw = a
                P.dma("pool", f"cv{slot}", lambda e: e.dma_start(
                    out=dst[:, 0:nk * cw].rearrange("p (kc c) -> p kc c", kc=nk),
                    in_=wsrc[src][k0 * 128:(k0 + nk) * 128, :].rearrange("(kc p) c -> p kc c", p=128)),
                    reads=[], writes=[rk])
            elif kind == "uqkv":
                P.dma("pool", f"cv{slot}", lambda e: e.dma_start(
                    out=dst[:, 0:1536].rearrange("p (kc c) -> p kc c", kc=2),
                    in_=w_uq.rearrange("(kc p) c -> p kc c", p=128)), reads=[], writes=[rk])
                P.dma("pool", f"cv{slot}", lambda e: e.dma_start(out=dst[:, 1536:2560], in_=w_ukv), reads=[], writes=[rk])
            elif kind == "wo":
                hf = a
                P.dma("pool", f"cv{slot}", lambda e: e.dma_start(
                    out=dst[0:64, 0:4096].rearrange("p (hh c) -> p hh c", hh=8),
                    in_=w_o_mla[:, hf * 512:(hf + 1) * 512].rearrange("(hh p) c -> p hh c", p=64)),
                    reads=[], writes=[rk])
            elif kind == "cdiag":
                cc = a
                for j in range(31):
                    ts_("dve" if j % 2 == 0 else "pool", dst[:, j * 128:(j + 1) * 128], ident_f[:], cwT[:, cc, j:j + 1], 1.0,
                        ALU.mult, ALU.mult, ["ident_f", "cwT"], [rk])
            P.dma("sp", f"xo{slot}", lambda e: e.dma_start(out=wb_d[pi], in_=dst[:, :]), reads=[rk], writes=[f"wb{pi}"])

        for pi in range(NP):
            fill_panel(pi, pi % RING)

        pname = {nm: i for i, (nm, _, _) in enumerate(panels)}
        PR = [P]
        stt_ = dict(seq=None, pos=0, rec=[], pend=[], free=list(range(RING)))

        def auto_prefetch():
            if stt_["seq"] is None:
                return
            while stt_["free"] and len(stt_["pend"]) < 2:
                nm = stt_["seq"][stt_["pos"] % len(stt_["seq"])]
                stt_["pos"] += 1
                slot = stt_["free"].pop(0)
                pi = pname[nm]
                PR[0].dma("sp", f"rl{slot}", lambda e, slot=slot, pi=pi: e.dma_start(out=ring[slot][:, :], in_=wb_d[pi]),
                          reads=[f"wb{pi}"], writes=[f"ring{slot}"])
                stt_["pend"].append((nm, slot))

        def take(nm):
            if stt_["seq"] is None:
                stt_["rec"].append(nm)
                return 0
            auto_prefetch()
            assert stt_["pend"], "panel ring exhausted"
            n2, s_ = stt_["pend"].pop(0)
            assert n2 == nm, (n2, nm)
            return s_

        def release(s_):
            if stt_["seq"] is None:
                return
            stt_["free"].append(s_)
            auto_prefetch()

        out_ops = []

        def mk_alloc(banks):
            st_ = [0]

            def f():
                b = banks[st_[0] % len(banks)]
                st_[0] += 1
                return b
            return f

        nb4 = mk_alloc([0, 1, 2, 3])
        nb8 = mk_alloc([0, 1, 2, 3, 4, 5, 6, 7])
        nbA = mk_alloc([0, 1])
        nbAt = mk_alloc([2])
        nbB = mk_alloc([4, 5, 6, 7])
        nbS = mk_alloc([0, 1, 2])
        nbL = mk_alloc([3])

        def pk(b):
            return [f"ps{b}"] if b < 4 else [f"ps{b}", f"ps{b}h0", f"ps{b}h64"]

        def hk(b, po):
            return f"ps{b}h{po}"

        def op(q_, fn, reads=(), writes=()):
            return PR[0].op(q_, fn, reads=reads, writes=writes)

        def mm(out, lhsT, rhs, start, stop, reads, writes):
            op("pe", lambda e: e.matmul(out, lhsT=lhsT, rhs=rhs, start=start, stop=stop), reads, writes)

        def tr(out, in_, idn, reads, writes):
            op("pe", lambda e: e.transpose(out, in_, idn), reads, writes)

        def act(out, in_, func, reads, writes, scale=1.0, bias=0.0, accum=None):
            if accum is None:
                op("act", lambda e: e.activation(out=out, in_=in_, func=func, bias=bias, scale=scale), reads, writes)
            else:
                op("act", lambda e: e.activation(out=out, in_=in_, func=func, bias=bias, scale=scale, accum_out=accum), reads, writes)

        def tt_(q_, out, in0, in1, op_, reads, writes):
            op(q_, lambda e: e.tensor_tensor(out=out, in0=in0, in1=in1, op=op_), reads, writes)

        def ts_(q_, out, in0, s1, s2, op0, op1, reads, writes):
            if op1 is None:
                op(q_, lambda e: e.tensor_scalar(out=out, in0=in0, scalar1=s1, scalar2=None, op0=op0), reads, writes)
            else:
                op(q_, lambda e: e.tensor_scalar(out=out, in0=in0, scalar1=s1, scalar2=s2, op0=op0, op1=op1), reads, writes)

        def stt(out, in0, scalar, in1, op0, op1, reads, writes, accum=None):
            if accum is None:
                op("dve", lambda e: e.scalar_tensor_tensor(out=out, in0=in0, scalar=scalar, in1=in1, op0=op0, op1=op1), reads, writes)
            else:
                op("dve", lambda e: e.scalar_tensor_tensor(out=out, in0=in0, scalar=scalar, in1=in1, op0=op0, op1=op1, accum_out=accum), reads, writes)

        def cp(q_, out, in_, reads, writes):
            if q_ == "act":
                op(q_, lambda e: e.activation(out=out, in_=in_, func=AF.Identity), reads, writes)
            else:
                op(q_, lambda e: e.tensor_copy(out=out, in_=in_), reads, writes)

        def rsqrt_pool(out, in_, mult, reads, writes, post=None):
            ts_("pool", out, in_, mult, EPS, ALU.mult, ALU.add, reads, writes)
            tt_("pool", out, out, cmh[:, 0:1].to_broadcast(list(out.shape)), ALU.pow, list(writes) + ["cmh"], writes)
            if post is not None:
                ts_("pool", out, out, post, 1.0, ALU.mult, ALU.mult, writes, writes)

        def run_streams(*gens):
            gens = [g for g in gens]
            while gens:
                for g in list(gens):
                    try:
                        next(g)
                    except StopIteration:
                        gens.remove(g)

        def norm_stats(tb):
            act(junk[:], xt[:, tb, :], AF.Square, [f"xt{tb}"], ["junk", f"ss{tb}"], accum=ss[:, tb:tb + 1])
            rsqrt_pool(rstd[:, tb:tb + 1], ss[:, tb:tb + 1], 1.0 / D, [f"ss{tb}"], [f"rstd{tb}"])

        def norm_apply(tb, gv_a, shv, tagks):
            xb = xs2[tb % 2]
            xk = f"xs{tb % 2}"
            ts_("dve", xb[:], xt[:, tb, :], rstd[:, tb:tb + 1], None, ALU.mult, None, [f"xt{tb}", f"rstd{tb}"], [xk])
            b = nb4()
            pv = psb(b)
            for kc in range(8):
                tr(pv[:, kc * 128:(kc + 1) * 128], xb[:, kc * 128:(kc + 1) * 128], ident[:], [xk, "ident"], [*pk(b)])
            for kc in range(8):
                if kc % 2 == 0 or not F_NORMDVE:
                    act(h[:, kc, tb * 128:(tb + 1) * 128], pv[:, kc * 128:(kc + 1) * 128], AF.Identity,
                        [*pk(b)] + tagks, ["h"], scale=gv_a[:, kc:kc + 1], bias=shv[:, kc:kc + 1])
                else:
                    ts_("dve", h[:, kc, tb * 128:(tb + 1) * 128], pv[:, kc * 128:(kc + 1) * 128], gv_a[:, kc:kc + 1], shv[:, kc:kc + 1],
                        ALU.mult, ALU.add, [*pk(b)] + tagks, ["h"])

        def norm_to_h(gv_a, shv, tagks):
            for tb in range(4):
                norm_stats(tb)
            for tb in range(4):
                norm_apply(tb, gv_a, shv, tagks)

        def gen_qkv(ti):
            s_lat = take("lat")
            lat = ring[s_lat]
            for tb in range(4):
                tsl = slice(tb * 128, (tb + 1) * 128)
                b = nbA()
                for kc in range(8):
                    mm(ps[b][:, 0:416], h[:, kc, tsl], lat[:, kc * 416:(kc + 1) * 416], kc == 0, kc == 7,
                       ["h", f"ring{s_lat}"], [*pk(b)])
                cp("dve", zlat[:], ps[b][:, 0:416], [*pk(b)], ["zlat"])
                yield
                act(junk2[:, 0:256], zlat[:, 0:256], AF.Square, ["zlat"], ["junk2", "st1"], accum=st1[:, 0:1])
                act(junk2[:, 256:384], zlat[:, 256:384], AF.Square, ["zlat"], ["junk2", "st1"], accum=st1[:, 1:2])
                act(junk2[:, 384:416], zlat[:, 384:416], AF.Square, ["zlat"], ["junk2", f"skr{tb}"], accum=sskr[:, tb:tb + 1])
                cp("pool", zkr[:, tb, :], zlat[:, 384:416], ["zlat"], [f"zkr{tb}"])
                rsqrt_pool(st2[:, 0:1], st1[:, 0:1], 1.0 / 256, ["st1"], ["st2"])
                rsqrt_pool(st2[:, 1:2], st1[:, 1:2], 1.0 / 128, ["st1"], ["st2"])
                yield
                stt(zn[:, 0:256], zlat[:, 0:256], st2[:, 0:1], gqlat_bc[:], ALU.mult, ALU.mult, ["zlat", "st2", "gqlat_bc"], ["zn"])
                stt(zn[:, 256:384], zlat[:, 256:384], st2[:, 1:2], gkvlat_bc[:], ALU.mult, ALU.mult, ["zlat", "st2", "gkvlat_bc"], ["zn"])
                b = nbAt()
                pv = psb(b)
                for c3 in range(3):
                    tr(pv[:, c3 * 128:(c3 + 1) * 128], zn[:, c3 * 128:(c3 + 1) * 128], ident[:], ["zn", "ident"], [*pk(b)])
                cp("dve", znT[:, :, tsl], pv[:, 0:384].rearrange("p (c t) -> p c t", c=3), [*pk(b)], [f"znT{tb}"])
                yield
            release(s_lat)
            s_uq = take("uqkv")
            uqp = ring[s_uq]
            for tb in range(4):
                blk = ti * 4 + tb
                tsl = slice(tb * 128, (tb + 1) * 128)
                bq = [0, 1]
                bk = [2, 3]
                for hf in range(2):
                    for kc in range(2):
                        mm(ps[bq[hf]][:, 0:384], znT[:, kc, tsl], uqp[:, kc * 768 + hf * 384: kc * 768 + hf * 384 + 384],
                           kc == 0, kc == 1, [f"znT{tb}", f"ring{s_uq}"], [*pk(bq[hf])])
                for hf in range(2):
                    mm(ps[bk[hf]][:, :], znT[:, 2, tsl], uqp[:, 1536 + hf * 512:1536 + (hf + 1) * 512], True, True,
                       [f"znT{tb}", f"ring{s_uq}"], [*pk(bk[hf])])
                for hf in range(2):
                    act(sq[:, hf * 384:(hf + 1) * 384], ps[bq[hf]][:, 0:384], AF.Square, [*pk(bq[hf])], ["sq"])
                for hf in range(2):
                    kv3 = ps[bk[hf]][:, :].rearrange("p (h d) -> p h d", h=4)
                    act(tmpA[:, hf * 256:(hf + 1) * 256].rearrange("p (h d) -> p h d", h=4), kv3[:, :, 0:64], AF.Square,
                        [*pk(bk[hf])], ["tmpA"])
                op("dve", lambda e: e.tensor_reduce(out=ssq[:], in_=sq[:].rearrange("p (h d) -> p h d", h=8), axis=AX.X, op=ALU.add),
                   ["sq"], ["ssq"])
                op("dve", lambda e: e.tensor_reduce(out=ssk[:], in_=tmpA[:, 0:512].rearrange("p (h d) -> p h d", h=8), axis=AX.X, op=ALU.add),
                   ["tmpA"], ["ssk"])
                rsqrt_pool(r2[:], ssq[:], 1.0 / 96, ["ssq"], ["r2"])
                ts_("pool", ssk[:], ssk[:], sskr[:, tb:tb + 1], 1.0, ALU.add, ALU.mult, ["ssk", f"skr{tb}"], ["ssk"])
                rsqrt_pool(kscale[:, blk, :], ssk[:], 1.0 / 96, ["ssk"], ["kscale"], post=96 ** -0.5)
                yield
                kb3 = kb[:].rearrange("p (h d) -> p h d", h=8)
                for hf in range(2):
                    kv3 = ps[bk[hf]][:, :].rearrange("p (h d) -> p h d", h=4)
                    tt_("dve", kb3[:, hf * 4:(hf + 1) * 4, 0:64], kv3[:, :, 0:64], AP(gk_bc, 0, [[96, 128], [0, 4], [1, 64]]), ALU.mult,
                        [*pk(bk[hf]), "gk_bc"], ["kb"])
                    cp("act", Vc[:, blk, hf * 256:(hf + 1) * 256].rearrange("p (h d) -> p h d", h=4), kv3[:, :, 64:128],
                       [*pk(bk[hf])], ["Vc"])
                tt_("dve", kr[:], zkr[:, tb, :], gk_bc[:, 64:96], ALU.mult, [f"zkr{tb}", "gk_bc"], ["kr"])
                c2 = cos_t[:, blk, :]
                s2 = sin_t[:, blk, :]
                tt_("dve", rt[:, 0, :], kr[:, 0:16], c2, ALU.mult, ["kr", "cos"], ["rt"])
                tt_("dve", rt2[:, 0, :], kr[:, 16:32], s2, ALU.mult, ["kr", "sin"], ["rt2"])
                tt_("dve", krb[:, 0:16], rt[:, 0, :], rt2[:, 0, :], ALU.subtract, ["rt", "rt2"], ["krb"])
                tt_("dve", rt[:, 0, :], kr[:, 16:32], c2, ALU.mult, ["kr", "cos"], ["rt"])
                tt_("dve", rt2[:, 0, :], kr[:, 0:16], s2, ALU.mult, ["kr", "sin"], ["rt2"])
                tt_("dve", krb[:, 16:32], rt[:, 0, :], rt2[:, 0, :], ALU.add, ["rt", "rt2"], ["krb"])
                cp("pool", kb3[:, :, 64:96], AP(krb, 0, [[32, 128], [0, 8], [1, 32]]), ["krb"], ["kb"])
                pvk = psb(2)
                for hh in range(8):
                    tr(pvk[0:96, hh * 128:(hh + 1) * 128], kb[:, hh * 96:(hh + 1) * 96], ident[:], ["kb", "ident"], [*pk(2)])
                cp("act" if F_ACTCP else "dve", Kc[:, :, blk * 128:(blk + 1) * 128], pvk[0:96, :].rearrange("p (h t) -> p h t", h=8), [*pk(2)], ["Kc"])
                yield
                for hf in range(2):
                    tt_("dve", q[:, hf * 384:(hf + 1) * 384].rearrange("p (h d) -> p h d", h=4),
                        ps[bq[hf]][:, 0:384].rearrange("p (h d) -> p h d", h=4),
                        AP(r2, hf * 4, [[8, 128], [1, 4], [0, 96]]), ALU.mult, [*pk(bq[hf]), "r2"], ["q"])
                q3 = q[:].rearrange("p (h d) -> p h d", h=8)
                tt_("dve", q3, q3, AP(gq_bc, 0, [[96, 128], [0, 8], [1, 96]]), ALU.mult, ["q", "gq_bc"], ["q"])
                qb3 = qb[:].rearrange("p (h d) -> p h d", h=8)
                cosb = AP(cos_t, blk * 16, [[256, 128], [0, 8], [1, 16]])
                sinb = AP(sin_t, blk * 16, [[256, 128], [0, 8], [1, 16]])
                cp("pool", qb3[:, :, 0:64], q3[:, :, 0:64], ["q"], ["qb"])
                tt_("dve", rt[:], q3[:, :, 64:80], cosb, ALU.mult, ["q", "cos"], ["rt"])
                tt_("dve", rt2[:], q3[:, :, 80:96], sinb, ALU.mult, ["q", "sin"], ["rt2"])
                tt_("dve", qb3[:, :, 64:80], rt[:], rt2[:], ALU.subtract, ["rt", "rt2"], ["qb"])
                tt_("dve", rt[:], q3[:, :, 80:96], cosb, ALU.mult, ["q", "cos"], ["rt"])
                tt_("dve", rt2[:], q3[:, :, 64:80], sinb, ALU.mult, ["q", "sin"], ["rt2"])
                tt_("dve", qb3[:, :, 80:96], rt[:], rt2[:], ALU.add, ["rt", "rt2"], ["qb"])
                yield
                pvq = psb(0)
                for hh in range(8):
                    tr(pvq[0:96, hh * 128:(hh + 1) * 128], qb[:, hh * 96:(hh + 1) * 96], ident[:], ["qb", "ident"], [*pk(0)])
                cp("act" if F_ACTCP else "dve", qm[0:96, :, tsl], pvq[0:96, :].rearrange("p (h t) -> p h t", h=8), [*pk(0)], ["qm"])
                yield
            release(s_uq)

        def gen_glu_gates_conv(ti):
            s_b = take("gluB")
            for c4 in range(4):
                bb = nbB()
                for kc in range(8):
                    mm(ps[bb][:, :], ring[s_b][:, kc * 512 + c4 * 128: kc * 512 + c4 * 128 + 128], h[:, kc, :], kc == 0, kc == 7,
                       [f"ring{s_b}", "h"], [*pk(bb)])
                act(tg[:, 12 + c4, :], ps[bb][:, :], AF.Tanh, [*pk(bb)], [f"tg{12 + c4}"], scale=0.5)
                yield
            release(s_b)
            s_a = take("gluA")
            for c4 in range(4):
                ba = nbB()
                for kc in range(8):
                    mm(ps[ba][:, :], ring[s_a][:, kc * 512 + c4 * 128: kc * 512 + c4 * 128 + 128], h[:, kc, :], kc == 0, kc == 7,
                       [f"ring{s_a}", "h"], [*pk(ba)])
                stt(uT[:, c4, 30:30 + T], tg[:, 12 + c4, :], 1.0, ps[ba][:, :], ALU.add, ALU.mult, [f"tg{12 + c4}", *pk(ba)], ["uT"])
                yield
            release(s_a)
            for g in range(4):
                s_g = take(f"gate{g}")
                for c4 in range(4):
                    b = nbB()
                    for kc in range(8):
                        mm(ps[b][:, :], ring[s_g][:, kc * 512 + c4 * 128: kc * 512 + c4 * 128 + 128], h[:, kc, :], kc == 0, kc == 7,
                           [f"ring{s_g}", "h"], [*pk(b)])
                    act(tg[:, g * 4 + c4, :], ps[b][:, :], AF.Tanh, [*pk(b)], [f"tg{g * 4 + c4}"], scale=0.5)
                    yield
                release(s_g)
            for cc in range(4):
                s_c = take(f"cdiag{cc}")
                for tb in range(4):
                    for j in range(31):
                        mm(ps[4 + tb][:, cc * 128:(cc + 1) * 128], uT[:, cc, tb * 128 + j: tb * 128 + j + 128],
                           ring[s_c][:, j * 128:(j + 1) * 128], j == 0, j == 30, ["uT", f"ring{s_c}"], [*pk(4 + tb)])
                    yield
                release(s_c)
            for tb in range(4):
                stt(cx4[:, tb, :], ps[4 + tb][:, :], 0.5, convb_bc[:], ALU.mult, ALU.add, [*pk(4 + tb), "convb_bc"], [f"cx{tb}", f"lnst{tb}"],
                    accum=lnst4[:, tb, 0:1])
            if ti == NT - 1:
                op("pool", lambda e: e.memset(uT[:, :, 0:30], 0.0), ["uT"], ["uT"])
            else:
                cp("pool", uT[:, :, 0:30], uT[:, :, T:T + 30], ["uT"], ["uT"])
            yield

        def gen_ln(ti):
            for tb in range(4):
                tsl = slice(tb * 128, (tb + 1) * 128)
                ln = lnst4[:, tb, :]
                lk = f"lnst{tb}"
                act(junk2[:, 0:512], cx4[:, tb, :], AF.Square, [f"cx{tb}"], ["junk2", lk], accum=ln[:, 1:2])
                ts_("pool", ln[:, 0:1], ln[:, 0:1], 1.0 / 512, 1.0, ALU.mult, ALU.mult, [lk], [lk])
                tt_("pool", ln[:, 2:3], ln[:, 0:1], ln[:, 0:1], ALU.mult, [lk], [lk])
                ts_("pool", ln[:, 1:2], ln[:, 1:2], 1.0 / 512, ln[:, 2:3], ALU.mult, ALU.subtract, [lk], [lk])
                rsqrt_pool(ln[:, 1:2], ln[:, 1:2], 1.0, [lk], [lk])
                tt_("pool", ln[:, 3:4], ln[:, 0:1], ln[:, 1:2], ALU.mult, [lk], [lk])
                ts_("pool", ln[:, 3:4], ln[:, 3:4], -1.0, 1.0, ALU.mult, ALU.mult, [lk], [lk])
                yield
                act(xn[:], cx4[:, tb, :], AF.Identity, [f"cx{tb}", lk], ["xn"], scale=ln[:, 1:2], bias=ln[:, 3:4])
                b = nbL()
                pv = psb(b)
                for c4 in range(4):
                    tr(pv[:, c4 * 128:(c4 + 1) * 128], xn[:, c4 * 128:(c4 + 1) * 128], ident[:], ["xn", "ident"], [*pk(b)])
                yield
                for c4 in range(4):
                    act(yy[:, c4 * 128:(c4 + 1) * 128], pv[:, c4 * 128:(c4 + 1) * 128], AF.Identity, [*pk(b), "lng", "lnb"], ["yy"],
                        scale=lng[:, c4:c4 + 1], bias=lnb[:, c4:c4 + 1])
                act(t2[:], yy[:], AF.Tanh, ["yy"], ["t2"], scale=0.5)
                stt(u2T[:, :, tsl], t2[:].rearrange("p (c t) -> p c t", c=4), 1.0, yy[:].rearrange("p (c t) -> p c t", c=4),
                    ALU.add, ALU.mult, ["t2", "yy"], ["u2T"])
                yield

        def gen_attn(ti):
            nkb = 4 * ti + 4
            for hh in range(NH):
                po = 64 * (hh % 2)
                bacc = 4 + ((hh // 2) % 2) * 2
                bden = bacc + 1
                info = {}

                def s_stage(kbi):
                    j = kbi - 4 * ti
                    c0 = 128 * j if j >= 0 else 0
                    b = nbS()
                    mm(ps[b][:, c0:T], Kc[:, hh, kbi * 128:(kbi + 1) * 128], qm[0:96, hh, c0:T], True, True,
                       ["Kc", "qm"], [*pk(b)])
                    ksc = kscale[:, kbi, hh:hh + 1]
                    if j < 0:
                        pt = PT[kbi % 4]
                        ptk = f"PT{kbi % 4}"
                        act(pt[:, :], ps[b][:, :], AF.Exp, [*pk(b), "kscale"], [ptk], scale=ksc)
                    else:
                        pt = PD[j]
                        ptk = f"PD{j}"
                        if c0 + 64 < T:
                            act(pt[:, c0 + 64:T], ps[b][:, c0 + 64:T], AF.Exp, [*pk(b), "kscale"], [ptk], scale=ksc)
                        act(pt[0:64, c0:c0 + 64], ps[b][0:64, c0:c0 + 64], AF.Exp, [*pk(b), "kscale"], [ptk], scale=ksc[0:64, :])
                    info[kbi] = (pt, ptk, c0)

                def pv_stage(kbi):
                    pt, ptk, c0 = info[kbi]
                    mm(ps[bacc][po:po + 64, c0:T], Vc[:, kbi, hh * 64:(hh + 1) * 64], pt[:, c0:T], kbi == 0, kbi == nkb - 1,
                       [ptk, "Vc"], [hk(bacc, po)])
                    mm(ps[bden][po:po + 64, c0:T], ones64[:, :], pt[:, c0:T], kbi == 0, kbi == nkb - 1,
                       [ptk, "ones64"], [hk(bden, po)])

                for idx in range(nkb + F_DEPTH):
                    if idx < nkb:
                        s_stage(idx)
                    if idx >= F_DEPTH:
                        pv_stage(idx - F_DEPTH)
                    yield
                op("dve", lambda e, bden=bden, po=po: e.reciprocal(out=recip[po:po + 64, :], in_=ps[bden][po:po + 64, :]),
                   [hk(bden, po)], [f"recip{po}"])
                tt_("dve", attnT[po:po + 64, hh // 2, :], ps[bacc][po:po + 64, :], recip[po:po + 64, :], ALU.mult,
                    [hk(bacc, po), f"recip{po}"], ["attnT"])
                yield

        def emit_tile(bi, ti):
            t0 = ti * T
            PR[0].phase = f"ld:{bi}:{ti}"
            if F_XTB:
                for tb in range(4):
                    PR[0].dma("sp", f"xl{tb}", lambda e, bi=bi, t0=t0, tb=tb: e.dma_start(
                        out=xt[:, tb, :], in_=x_d[bi, t0 + tb * 128:t0 + (tb + 1) * 128, :]),
                        reads=[], writes=[f"xt{tb}"])
            else:
                PR[0].dma("sp", "xl0", lambda e, bi=bi, t0=t0: e.dma_start(
                    out=xt[:], in_=x_d[bi, t0:t0 + T, :].rearrange("(tb p) d -> p tb d", p=128)),
                    reads=[], writes=[f"xt{tb}" for tb in range(4)])
            PR[0].phase = f"norm1:{bi}:{ti}"
            norm_to_h(a1, sh1, ["a1", "sh1"])
            PR[0].phase = f"qkvB:{bi}:{ti}"
            run_streams(gen_qkv(ti), gen_glu_gates_conv(ti))
            PR[0].phase = f"attn:{bi}:{ti}"
            run_streams(gen_attn(ti), gen_ln(ti))
            PR[0].phase = f"merge:{bi}:{ti}"
            s_pw = take("pw")
            s_o = take("wo")
            for mc in range(8):
                ba, bb = nb8(), nb8()
                for kc in range(4):
                    mm(ps[ba][:, :], ring[s_o][:, kc * 1024 + mc * 128: kc * 1024 + mc * 128 + 128], attnT[:, kc, :],
                       kc == 0, kc == 3, [f"ring{s_o}", "attnT"], [*pk(ba)])
                for kc in range(4):
                    mm(ps[bb][:, :], ring[s_pw][:, kc * 1024 + mc * 128: kc * 1024 + mc * 128 + 128], u2T[:, kc, :],
                       kc == 0, kc == 3, [f"ring{s_pw}", "u2T"], [*pk(bb)])
                stt(tmpA[:], tg[:, mc, :], 1.0, ps[ba][:, :], ALU.add, ALU.mult, [f"tg{mc}", *pk(ba)], ["tmpA"])
                stt(tmpB[:], tg[:, 8 + mc, :], 1.0, ps[bb][:, :], ALU.add, ALU.mult, [f"tg{8 + mc}", *pk(bb)], ["tmpB"])
                stt(qm[:, mc, :], tmpB[:], 0.5, tmpA[:], ALU.mult, ALU.add, ["tmpA", "tmpB"], ["qm"])
            release(s_pw)
            release(s_o)
            PR[0].phase = f"wout:{bi}:{ti}"
            if F_WOUT:
                s_w = [take("wout0"), take("wout1")]
                for tb in range(4):
                    for hf in range(2):
                        b = nb8()
                        for kc in range(8):
                            mm(ps[b][:, :], qm[:, kc, tb * 128:(tb + 1) * 128], ring[s_w[hf]][:, kc * 512:(kc + 1) * 512], kc == 0, kc == 7,
                               ["qm", f"ring{s_w[hf]}"], [*pk(b)])
                        tb_ = tmpB if hf == 0 else tmpA
                        tk = "tmpB" if hf == 0 else "tmpA"
                        tt_("dve", tb_[:], ps[b][:, :], gate1h[:, hf * 512:(hf + 1) * 512], ALU.mult, [*pk(b), "gate1h"], [tk])
                        tt_("pool" if hf == 0 else "dve", xt[:, tb, hf * 512:(hf + 1) * 512], xt[:, tb, hf * 512:(hf + 1) * 512], tb_[:], ALU.add,
                            [f"xt{tb}", tk], [f"xt{tb}"])
                    norm_stats(tb)
                    if tb >= 1:
                        norm_apply(tb - 1, a2, sh2, ["a2", "sh2"])
                release(s_w[0])
                release(s_w[1])
                norm_apply(3, a2, sh2, ["a2", "sh2"])

            else:
                for hf in range(2):
                    s_w1 = take(f"wout{hf}")
                    for tb in range(4):
                        b = nb8()
                        for kc in range(8):
                            mm(ps[b][:, :], qm[:, kc, tb * 128:(tb + 1) * 128], ring[s_w1][:, kc * 512:(kc + 1) * 512], kc == 0, kc == 7,
                               ["qm", f"ring{s_w1}"], [*pk(b)])
                        tt_("dve", tmpA[:], ps[b][:, :], gate1h[:, hf * 512:(hf + 1) * 512], ALU.mult, [*pk(b), "gate1h"], ["tmpA"])
                        tt_("dve", xt[:, tb, hf * 512:(hf + 1) * 512], xt[:, tb, hf * 512:(hf + 1) * 512], tmpA[:], ALU.add,
                            [f"xt{tb}", "tmpA"], [f"xt{tb}"])
                    release(s_w1)
                PR[0].phase = f"norm2:{bi}:{ti}"
                norm_to_h(a2, sh2, ["a2", "sh2"])
            PR[0].phase = f"ffn:{bi}:{ti}"
            for dh in range(2):
                for p in range(4):
                    s_f = take(f"ff1_{dh}_{p}")
                    for c4 in range(4):
                        b = nb8()
                        for kc in range(8):
                            mm(ps[b][:, :], ring[s_f][:, kc * 512 + c4 * 128: kc * 512 + c4 * 128 + 128], h[:, kc, :], kc == 0, kc == 7,
                               [f"ring{s_f}", "h"], [*pk(b)])
                        tb_ = tmpB if c4 % 2 == 0 else tmpA
                        tk = "tmpB" if c4 % 2 == 0 else "tmpA"
                        act(tb_[:], ps[b][:, :], AF.Relu, [*pk(b)], [tk])
                        tt_("dve", tg[:, p * 4 + c4, :], tb_[:], tb_[:], ALU.mult, [tk], [f"tg{p * 4 + c4}"])
                    release(s_f)
                for p in range(4):
                    s_f = take(f"ff2_{dh}_{p}")
                    for k4 in range(4):
                        kc = p * 4 + k4
                        for tb in range(4):
                            for hf in range(2):
                                bo = tb * 2 + hf
                                mm(ps[bo][:, :], tg[:, kc, tb * 128:(tb + 1) * 128],
                                   ring[s_f][:, k4 * 1024 + hf * 512: k4 * 1024 + hf * 512 + 512], kc == 0, kc == 15,
                                   [f"tg{kc}", f"ring{s_f}"], [*pk(bo)])
                    release(s_f)
                for tb in range(4):
                    for hf in range(2):
                        bo = tb * 2 + hf
                        tb_ = tmpB if hf % 2 == 0 else tmpA
                        tk = "tmpB" if hf % 2 == 0 else "tmpA"
                        tt_("dve", tb_[:], ps[bo][:, :], gate2b[:, hf * 512:(hf + 1) * 512], ALU.mult, [*pk(bo), "gate2b"], [tk])
                        tt_("pool" if dh == 0 else "dve", xt[:, tb, hf * 512:(hf + 1) * 512], xt[:, tb, hf * 512:(hf + 1) * 512], tb_[:], ALU.add,
                            [f"xt{tb}", tk], [f"xt{tb}"])
            PR[0].phase = f"st:{bi}:{ti}"
            if F_XTB:
                for tb in range(4):
                    oi = PR[0].dma("sp", f"xo{tb}", lambda e, bi=bi, t0=t0, tb=tb: e.dma_start(
                        out=out_d[bi, t0 + tb * 128:t0 + (tb + 1) * 128, :], in_=xt[:, tb, :]),
                        reads=[f"xt{tb}"], writes=["out_d"])
                    out_ops.append(oi)
            else:
                oi = PR[0].dma("sp", "xo0", lambda e, bi=bi, t0=t0: e.dma_start(
                    out=out_d[bi, t0:t0 + T, :].rearrange("(tb p) d -> p tb d", p=128), in_=xt[:]),
                    reads=[f"xt{tb}" for tb in range(4)], writes=["out_d"])
                out_ops.extend([oi] * 4)

        PR[0] = Prog(nc)
        emit_tile(0, 1)
        stt_["seq"] = list(stt_["rec"])
        assert sorted(stt_["seq"]) == sorted(pname.keys()), (stt_["seq"], list(pname.keys()))
        PR[0] = P
        out_ops.clear()

        for bi in range(BPC):
            def mrow(k):
                return mod_d[bi, k * D:(k + 1) * D]
            P.phase = f"seq:{bi}"
            sload(sh1[:], mrow(0).rearrange("(kc p) -> p kc", p=128), ["sh1"], reads=["mod_d"])
            sload(sc1[:], mrow(1).rearrange("(kc p) -> p kc", p=128), ["sc1"], reads=["mod_d"])
            sload(sh2[:], mrow(3).rearrange("(kc p) -> p kc", p=128), ["sh2"], reads=["mod_d"])
            sload(sc2[:], mrow(4).rearrange("(kc p) -> p kc", p=128), ["sc2"], reads=["mod_d"])
            P.dma("sp", "sl0", lambda e, bi=bi: e.dma_start(out=gate1h[:], in_=AP(mod_d.tensor, bi * 6 * D + 2 * D, [[0, 128], [1, D]])),
                  reads=["mod_d"], writes=["gate1h"])
            P.dma("sp", "sl1", lambda e, bi=bi: e.dma_start(out=gate2b[:], in_=AP(mod_d.tensor, bi * 6 * D + 5 * D, [[0, 128], [1, D]])),
                  reads=["mod_d"], writes=["gate2b"])
            ts_("dve", gate1h[:], gate1h[:], 0.5, None, ALU.mult, None, ["gate1h"], ["gate1h"])
            stt(a1[:], sc1[:], 1.0, g1v[:], ALU.add, ALU.mult, ["sc1", "g1v"], ["a1"])
            stt(a2[:], sc2[:], 1.0, g2v[:], ALU.add, ALU.mult, ["sc2", "g2v"], ["a2"])
            for ti in range(NT):
                emit_tile(bi, ti)

        P.emit(final_wait_ops=out_ops[-4:])
    return nc, P


def _rope_tables():
    inv_freq = (np.float32(10000.0) ** (-np.arange(0, 32, 2, dtype=np.float32) / np.float32(32))).astype(np.float32)
    ang = (np.arange(S, dtype=np.float32)[:, None] * inv_freq[None, :]).astype(np.float32)
    return np.cos(ang).astype(np.float32), np.sin(ang).astype(np.float32)


_CACHE = {}


def kernel(**inputs):
    if "nc" not in _CACHE:
        _CACHE["nc"] = build_nc()
    nc, _ = _CACHE["nc"]
    cos, sin = _rope_tables()
    ident = np.eye(128, dtype=np.float32)
    x = np.ascontiguousarray(np.asarray(inputs["x"], dtype=np.float32))
    c = np.ascontiguousarray(np.asarray(inputs["c"], dtype=np.float32))
    shared = {}
    for k, v in inputs.items():
        if k in ("x", "c"):
            continue
        a = np.asarray(v, dtype=np.float32)
        shared[k] = np.ascontiguousarray(a[0] if a.ndim == 3 else a)
    shared["rope_cos"] = cos
    shared["rope_sin"] = sin
    shared["ident"] = ident
    in_maps = []
    for i in range(NCORES):
        m = dict(shared)
        m["x"] = x[i * BPC:(i + 1) * BPC]
        m["c"] = c[i * BPC:(i + 1) * BPC]
        in_maps.append(m)
    res = run_bass_kernel_spmd(nc, in_maps, core_ids=list(range(NCORES)))
    return np.concatenate([r["out"] for r in res.results], axis=0)
```
